# Optimizing a Trainium2 kernel written in Bass

```python
import jax, jax.numpy as jnp
from jax import lax
import numpy as np

D_MODEL = 1024
BATCH = 8
SEQ = 4096
DEPTH = 4

N_A = DEPTH // 2
N_B = DEPTH - N_A
HEAD_DIM = 64
MIX_WIDTH = D_MODEL
MEM_LEN = 256
MEM_HEADS = 4
MEM_WIDTH = MEM_HEADS * HEAD_DIM
MAIN_WIDTH = MIX_WIDTH - MEM_WIDTH
POOL_WINDOWS = (2, 4, 8, 16)
POOL_GROUPS = len(POOL_WINDOWS)
POOL_GROUP_DIM = MAIN_WIDTH // POOL_GROUPS
SWA_Q_HEADS = MAIN_WIDTH // HEAD_DIM
SWA_KV_HEADS = 4
SWA_GROUP = SWA_Q_HEADS // SWA_KV_HEADS
KV_WIDTH = 2 * SWA_KV_HEADS * HEAD_DIM
MEM_KV_WIDTH = 2 * MEM_WIDTH
WINDOW = 128
BLOCK = 128
D_FF = 4 * D_MODEL
EPS = 1e-6

kernel_name = "yoco_pool_swa_sink_hybrid"


def rmsnorm(x, g):
    xf = x.astype(jnp.float32)
    y = xf * lax.rsqrt(jnp.mean(xf * xf, axis=-1, keepdims=True) + EPS)
    return (y * g.astype(jnp.float32)).astype(x.dtype)


def alibi_slopes(n):
    return jnp.exp2(-8.0 * jnp.arange(1, n + 1, dtype=jnp.float32) / n)


def pool_mixer(u, pool_w, pool_scale):
    B, S, _ = u.shape
    uf = u.astype(jnp.float32).reshape(B, S, POOL_GROUPS, POOL_GROUP_DIM)
    csum = jnp.concatenate([jnp.zeros((B, 1, POOL_GROUPS, POOL_GROUP_DIM), jnp.float32),
                            jnp.cumsum(uf, axis=1)], axis=1)
    win = jnp.array(POOL_WINDOWS, jnp.int32)
    t = jnp.arange(S, dtype=jnp.int32)[:, None]
    lo = jnp.maximum(t + 1 - win[None, :], 0)
    cnt = jnp.minimum(t + 1, win[None, :]).astype(jnp.float32)
    window_sum = csum[:, 1:] - csum[:, lo, jnp.arange(POOL_GROUPS)[None, :]]
    d = (window_sum / cnt[None, :, :, None] - uf).astype(u.dtype)
    mixed = jnp.einsum('bsgc,gcd->bsgd', d, pool_w)
    return mixed.reshape(B, S, MAIN_WIDTH) * pool_scale


def swa_sink_attention(q, k, v, sinks):
    B, S = q.shape[0], q.shape[1]
    nb = S // BLOCK
    qb = q.reshape(B, nb, BLOCK, SWA_KV_HEADS, SWA_GROUP, HEAD_DIM)

    def with_prev(a):
        ab = a.reshape(B, nb, BLOCK, SWA_KV_HEADS, HEAD_DIM)
        prev = jnp.pad(ab[:, :-1], ((0, 0), (1, 0), (0, 0), (0, 0), (0, 0)))
        return jnp.concatenate([prev, ab], axis=2)

    kb, vb = with_prev(k), with_prev(v)
    s = jnp.einsum('bnqkgd,bnpkd->bnkgqp', qb, kb).astype(jnp.float32) * (HEAD_DIM ** -0.5)
    blk = jnp.arange(nb, dtype=jnp.int32)[:, None] * BLOCK
    qpos = blk + jnp.arange(BLOCK, dtype=jnp.int32)[None, :]
    kpos = blk - BLOCK + jnp.arange(2 * BLOCK, dtype=jnp.int32)[None, :]
    dist = qpos[:, :, None] - kpos[:, None, :]
    valid = (dist >= 0) & (dist < WINDOW) & (kpos[:, None, :] >= 0)
    slopes = alibi_slopes(SWA_Q_HEADS).reshape(SWA_KV_HEADS, SWA_GROUP)
    s = s - slopes[None, None, :, :, None, None] * dist.astype(jnp.float32)[None, :, None, None]
    s = jnp.where(valid[None, :, None, None], s, jnp.finfo(jnp.float32).min)
    sink = sinks.astype(jnp.float32).reshape(SWA_KV_HEADS, SWA_GROUP)[None, None, :, :, None, None]
    m = jnp.maximum(jnp.max(s, axis=-1, keepdims=True), sink)
    e = jnp.exp(s - m)
    p = e / (jnp.sum(e, axis=-1, keepdims=True) + jnp.exp(sink - m))
    o = jnp.einsum('bnkgqp,bnpkd->bnqkgd', p.astype(vb.dtype), vb)
    return o.reshape(B, S, SWA_Q_HEADS * HEAD_DIM)


def memory_attention(q, mk, mv):
    s = jnp.einsum('bshd,bmhd->bhsm', q, mk).astype(jnp.float32) * (HEAD_DIM ** -0.5)
    p = jax.nn.softmax(s, axis=-1)
    o = jnp.einsum('bhsm,bmhd->bshd', p.astype(mv.dtype), mv)
    return o.reshape(q.shape[0], q.shape[1], MEM_WIDTH)


def sq_relu_mlp(x, w_up, w_down):
    h = jax.nn.relu(x @ w_up)
    return (h * h) @ w_down


def setup_inputs(seed: int = 0) -> dict:
    key = jax.random.key(seed)
    ks = jax.random.split(key, 20)
    f32 = jnp.float32

    def nrm(k, shape, scale):
        return jax.random.normal(k, shape, f32) * scale

    def gain(k, shape):
        return 1.0 + 0.02 * jax.random.normal(k, shape, f32)

    return {
        "x": nrm(ks[0], (BATCH, SEQ, D_MODEL), 1.0),
        "mem": nrm(ks[1], (BATCH, MEM_LEN, D_MODEL), 1.0),
        "norm_mix": gain(ks[2], (DEPTH, D_MODEL)),
        "w_in": nrm(ks[3], (DEPTH, D_MODEL, MIX_WIDTH), D_MODEL ** -0.5),
        "pool_w": nrm(ks[4], (N_A, POOL_GROUPS, POOL_GROUP_DIM, POOL_GROUP_DIM), POOL_GROUP_DIM ** -0.5),
        "pool_scale": gain(ks[5], (N_A, MAIN_WIDTH)),
        "kv_norm": gain(ks[6], (D_MODEL,)),
        "w_kv": nrm(ks[7], (D_MODEL, KV_WIDTH), D_MODEL ** -0.5),
        "k_norm": gain(ks[8], (HEAD_DIM,)),
        "q_norm": gain(ks[9], (N_B, HEAD_DIM)),
        "sinks": nrm(ks[10], (N_B, SWA_Q_HEADS), 0.5),
        "mem_norm": gain(ks[11], (DEPTH, D_MODEL)),
        "w_mem_kv": nrm(ks[12], (DEPTH, D_MODEL, MEM_KV_WIDTH), D_MODEL ** -0.5),
        "mem_q_norm": gain(ks[13], (DEPTH, HEAD_DIM)),
        "mem_k_norm": gain(ks[14], (DEPTH, HEAD_DIM)),
        "w_out": nrm(ks[15], (DEPTH, MIX_WIDTH, D_MODEL), MIX_WIDTH ** -0.5),
        "norm_mlp": gain(ks[16], (DEPTH, D_MODEL)),
        "w_up": nrm(ks[17], (DEPTH, D_MODEL, D_FF), D_MODEL ** -0.5),
        "w_down": nrm(ks[18], (DEPTH, D_FF, D_MODEL), D_FF ** -0.5),
    }


def reference(x, mem, norm_mix, w_in, pool_w, pool_scale, kv_norm, w_kv, k_norm, q_norm, sinks,
              mem_norm, w_mem_kv, mem_q_norm, mem_k_norm, w_out, norm_mlp, w_up, w_down):
    B, S, _ = x.shape
    h = x
    k_shared = None
    v_shared = None
    for l in range(DEPTH):
        if l == N_A:
            kv = rmsnorm(h, kv_norm) @ w_kv
            k_shared = rmsnorm(kv[..., :KV_WIDTH // 2].reshape(B, S, SWA_KV_HEADS, HEAD_DIM), k_norm)
            v_shared = kv[..., KV_WIDTH // 2:].reshape(B, S, SWA_KV_HEADS, HEAD_DIM)

        proj = rmsnorm(h, norm_mix[l]) @ w_in[l]
        main, mq = proj[..., :MAIN_WIDTH], proj[..., MAIN_WIDTH:]
        if l < N_A:
            main_out = pool_mixer(main, pool_w[l], pool_scale[l])
        else:
            j = l - N_A
            q = rmsnorm(main.reshape(B, S, SWA_Q_HEADS, HEAD_DIM), q_norm[j])
            main_out = swa_sink_attention(q, k_shared, v_shared, sinks[j])

        mkv = rmsnorm(mem, mem_norm[l]) @ w_mem_kv[l]
        mk = rmsnorm(mkv[..., :MEM_WIDTH].reshape(B, MEM_LEN, MEM_HEADS, HEAD_DIM), mem_k_norm[l])
        mv = mkv[..., MEM_WIDTH:].reshape(B, MEM_LEN, MEM_HEADS, HEAD_DIM)
        mqh = rmsnorm(mq.reshape(B, S, MEM_HEADS, HEAD_DIM), mem_q_norm[l])
        mem_out = memory_attention(mqh, mk, mv)

        h = h + jnp.concatenate([main_out, mem_out], axis=-1) @ w_out[l]
        h = h + sq_relu_mlp(rmsnorm(h, norm_mlp[l]), w_up[l], w_down[l])
    return h
```

```python
import math
import numpy as np
import concourse.bass as bass
import concourse.mybir as mybir
from concourse.bass_utils import run_bass_kernel_spmd

F32 = mybir.dt.float32
BF16 = mybir.dt.bfloat16
ALU = mybir.AluOpType
AF = mybir.ActivationFunctionType

ENGS = ("pe", "act", "dve", "pool", "sp")

D = 1024
SEQ = 4096
NB = 8
TILE = 1024
NT = SEQ // TILE
SUB = 512
MEM = 256
EPS = 1e-6
N_A = 2
SLOPES = [2.0 ** (-8.0 * (i + 1) / 12.0) for i in range(12)]
POOL_W = (2, 4, 8, 16)
NSLOT = 6
POOL_ENG_T1 = "dve"
SLOT_ELEMS = 4096

PC_NMIX = 0
PC_NMLP = 32
PC_MEMN = 64
PC_KVN = 96
PC_PSC = 104
PC_QN = 116
PC_KN = 118
PC_MQN = 119
PC_MKN = 123
PC_SINK = 127
NPAR = 144


class Op:
    __slots__ = ("eng", "fn", "deps", "sig", "cnt", "dsem")

    def __init__(self, eng, fn, dsem):
        self.eng = eng
        self.fn = fn
        self.dsem = dsem
        self.deps = ()
        self.sig = False
        self.cnt = 0


class Prog:
    def __init__(self):
        self.ops = {e: [] for e in ENGS}
        self.lastw = {}
        self.readers = {}
        self.dsem_last = {}

    def add(self, eng, fn, r=(), w=(), dsem=None):
        op = Op(eng, fn, dsem)
        deps = set()
        for k in r:
            lw = self.lastw.get(k)
            if lw is not None:
                deps.add(lw)
        for k in w:
            lw = self.lastw.get(k)
            if lw is not None:
                deps.add(lw)
            rs = self.readers.get(k)
            if rs:
                deps.update(rs)
        for k in w:
            self.lastw[k] = op
            self.readers[k] = []
        for k in r:
            if k not in w:
                self.readers.setdefault(k, []).append(op)
        if dsem is not None:
            prev = self.dsem_last.get(dsem)
            if prev is not None:
                deps.add(prev)
            self.dsem_last[dsem] = op
        deps.discard(op)
        op.deps = deps
        for d in deps:
            if d.dsem is None and not (d.eng == "pe" and eng == "pe"):
                d.sig = True
        self.ops[eng].append(op)
        return op

    def emit(self, block, esems, dsems):
        for e in ENGS:
            n = 0
            for op in self.ops[e]:
                if op.dsem is None and op.sig:
                    n += 1
                    op.cnt = n
        dcount = {}
        for e in ENGS:
            for op in self.ops[e]:
                if op.dsem is not None:
                    dcount[op.dsem] = dcount.get(op.dsem, 0) + 16
                    op.cnt = dcount[op.dsem]

        def run(e, eh):
            known = {}
            for op in self.ops[e]:
                waits = {}
                for d in op.deps:
                    if d.dsem is not None:
                        key = ("d", d.dsem)
                    else:
                        if d.eng == "pe" and e == "pe":
                            continue
                        key = ("e", d.eng)
                    if d.cnt > waits.get(key, 0):
                        waits[key] = d.cnt
                for key, v in waits.items():
                    if known.get(key, 0) >= v:
                        continue
                    known[key] = v
                    sem = dsems[key[1]] if key[0] == "d" else esems[key[1]]
                    eh.wait_ge(sem, v)
                ins = op.fn(eh) if op.fn is not None else None
                if ins is not None:
                    if op.dsem is not None:
                        ins.then_inc(dsems[op.dsem], 16)
                    elif op.sig:
                        ins.then_inc(esems[e], 1)
                else:
                    assert op.dsem is None and not op.sig

        @block.tensor
        def _(eh):
            run("pe", eh)

        @block.scalar
        def _(eh):
            run("act", eh)

        @block.vector
        def _(eh):
            run("dve", eh)

        @block.gpsimd
        def _(eh):
            run("pool", eh)

        @block.sync
        def _(eh):
            run("sp", eh)


def _kc_layout(w, ncols):
    K, N = w.shape
    kcn = K // 128
    a = w.reshape(kcn, 128, N // ncols, ncols).transpose(2, 1, 0, 3)
    return np.ascontiguousarray(a).reshape(N // ncols, 128, kcn * ncols)


def _prep_shared(inp):
    f = lambda a: np.asarray(a, dtype=np.float32)
    w_in, w_out, w_up, w_down = f(inp["w_in"]), f(inp["w_out"]), f(inp["w_up"]), f(inp["w_down"])
    pool_w, w_kv, w_mem_kv = f(inp["pool_w"]), f(inp["w_kv"]), f(inp["w_mem_kv"])
    sh = {}
    sh["win"] = np.stack([_kc_layout(w_in[l], 512) for l in range(4)])
    sh["wout"] = np.stack([_kc_layout(w_out[l], 512) for l in range(4)])
    sh["wup"] = np.stack([_kc_layout(w_up[l], 512) for l in range(4)])
    sh["wdn"] = np.stack([np.stack([_kc_layout(w_down[l][hf * 2048:(hf + 1) * 2048], 256) for hf in range(2)])
                          for l in range(4)])
    wp = []
    for l in range(N_A):
        bd = np.zeros((768, 768), np.float32)
        for g in range(4):
            bd[192 * g:192 * (g + 1), 192 * g:192 * (g + 1)] = pool_w[l, g]
        wp.append(_kc_layout(bd, 384))
    sh["wpool"] = np.stack(wp)
    kpart = w_kv[:, :256].reshape(1024, 4, 1, 64)
    kdup = np.concatenate([kpart, kpart], axis=2).reshape(1024, 512)
    sh["wkd"] = _kc_layout(kdup, 512)[0]
    sh["wv"] = _kc_layout(np.ascontiguousarray(w_kv[:, 256:]), 256)[0]
    sh["wmem"] = np.stack([_kc_layout(w_mem_kv[l], 512)[0] for l in range(4)])

    par = np.zeros((128, NPAR), np.float32)
    p = np.arange(128)

    def chunked(vec, col0):
        v = vec.reshape(-1, 128)
        for c in range(v.shape[0]):
            par[:, col0 + c] = v[c]

    for l in range(4):
        chunked(f(inp["norm_mix"])[l], PC_NMIX + 8 * l)
        chunked(f(inp["norm_mlp"])[l], PC_NMLP + 8 * l)
        chunked(f(inp["mem_norm"])[l], PC_MEMN + 8 * l)
        par[:, PC_MQN + l] = f(inp["mem_q_norm"])[l][p % 64]
        par[:, PC_MKN + l] = f(inp["mem_k_norm"])[l][p % 64]
    chunked(f(inp["kv_norm"]), PC_KVN)
    for l in range(N_A):
        chunked(f(inp["pool_scale"])[l], PC_PSC + 6 * l)
    for j in range(2):
        par[:, PC_QN + j] = f(inp["q_norm"])[j][p % 64]
        for c in range(6):
            par[:, PC_SINK + 6 * j + c] = f(inp["sinks"])[j][2 * c + p // 64]
    par[:, PC_KN] = f(inp["k_norm"])[p % 64]
    sh["params"] = par

    cst = np.zeros((128, 576), np.float32)
    r = np.arange(128)[:, None]
    c = np.arange(256)[None, :]
    dist = (c - r).astype(np.float32)
    cst[:, 0:256] = dist
    cst[:, 256:512] = ((dist >= 0) & (dist < 128)).astype(np.float32)
    for g, w in enumerate(POOL_W):
        cst[:, 512 + 16 * g:512 + 16 * (g + 1)] = 1.0 / np.minimum(np.arange(16) + 1, w).astype(np.float32)
    sh["consts"] = cst
    return sh


def build(l0, l1, n_tiles=NT):
    nc = bass.Bass("TRN2", target_bir_lowering=False)
    dr = lambda name, shape, kind="ExternalInput": nc.dram_tensor(name, shape, F32, kind=kind).ap()
    xT_d = dr("xT", [D, SEQ])
    memT_d = dr("memT", [D, MEM])
    par_d = dr("params", [128, NPAR])
    cst_d = dr("consts", [128, 576])
    win_d = dr("win", [4, 2, 128, 4096])
    wout_d = dr("wout", [4, 2, 128, 4096])
    wup_d = dr("wup", [4, 8, 128, 4096])
    wdn_d = dr("wdn", [4, 2, 4, 128, 4096])
    wpool_d = dr("wpool", [2, 2, 128, 2304])
    wkd_d = dr("wkd", [128, 4096])
    wv_d = dr("wv", [128, 2048])
    wmem_d = dr("wmem", [4, 128, 4096])
    yT_d = dr("yT", [D, SEQ], kind="ExternalOutput")

    P = Prog()
    layers = list(range(l0, l1))
    do_kv = (l0 <= 2 < l1) or (l0 >= 2)
    has_b = any(l >= N_A for l in layers)
    kv_layer = max(l0, 2)

    ARENA_BYTES = 212736
    cm = nc.sbuf_tensor("arena", [128, ARENA_BYTES // 4], F32)
    arena = cm.__enter__()
    off = [0]

    def alloc(n, dt):
        nb = n * (2 if dt == BF16 else 4)
        nb4 = (nb + 3) // 4
        o = off[0]
        off[0] += nb4
        assert off[0] * 4 <= ARENA_BYTES, f"SBUF arena overflow {off[0] * 4}"
        v = arena[:, o:o + nb4]
        return v.bitcast(dt) if dt != F32 else v

    h_sb = alloc(8 * TILE, F32)
    xn_sb = alloc(8 * TILE, BF16)
    ring = [alloc(SLOT_ELEMS, BF16) for _ in range(NSLOT)]
    mk_sb = alloc(4 * 2 * 256, BF16)
    mv_sb = alloc(4 * 2 * 256, BF16)
    kd_sb = alloc(4 * 1152, BF16)
    v_sb = alloc(9 * 256, BF16)
    etab = alloc(12 * 256, F32)
    par_sb = alloc(NPAR, F32)
    esink = alloc(12, F32)
    invcnt = alloc(64, F32)
    epsc = alloc(1, F32)
    _pad = alloc(1, F32)
    ones_bf = alloc(128, BF16)
    bd_bf = alloc(128, BF16)
    ones64 = alloc(64, BF16)
    carry = alloc(2 * 6 * 16, F32)
    sq_t = [alloc(512, BF16) for _ in range(2)]
    qsq_t = [alloc(512, BF16) for _ in range(2)]
    rs_t = [alloc(512, F32) for _ in range(2)]
    pt_t = [alloc(512, BF16) for _ in range(4)]
    ex_t = [alloc(512, F32) for _ in range(2)]
    keep_t = [alloc(512, BF16) for _ in range(2)]
    rec_t = [alloc(512, F32) for _ in range(2)]
    rl_t = [alloc(512, F32) for _ in range(2)]
    sab = [alloc(528, F32) for _ in range(2)]
    sab2 = [alloc(528, F32) for _ in range(2)]
    C_BYTES = 41984
    c_off = off[0]
    off[0] += C_BYTES // 4
    assert off[0] * 4 <= ARENA_BYTES, f"SBUF arena overflow {off[0] * 4}"

    def cview(byte_off, n, dt):
        nb = n * (2 if dt == BF16 else 4)
        v = arena[:, c_off + byte_off // 4: c_off + (byte_off + nb) // 4]
        return v.bitcast(dt) if dt != F32 else v

    def ck(byte_lo, byte_hi):
        return [("C", s) for s in range(byte_lo // 64, (byte_hi - 1) // 64 + 1)]

    hid_sb = cview(0, 16 * TILE, BF16)
    u_sb = cview(0, 6 * 1040, F32)
    D_OFF = 25600
    d_sb = cview(D_OFF, 6 * TILE, BF16)
    Q67_OFF = D_OFF + 12288
    q67_sb = cview(Q67_OFF, 2 * TILE, BF16)
    q_sb = cview(0, 8 * TILE, BF16)

    pcms = [nc.psum_tensor("ps_all", [128, 4096], F32)]
    ps_all = pcms[0].__enter__()[:]
    banks = [ps_all[:, b * 512:(b + 1) * 512] for b in range(8)]
    free_banks = list(range(8))

    def newbank():
        b = free_banks.pop(0)
        free_banks.append(b)
        return banks[b], ("B", b)

    def holdbank():
        b = free_banks.pop(0)
        return banks[b], ("B", b), b

    def relbank(b):
        free_banks.append(b)

    def holdall():
        assert len(free_banks) == 8
        del free_banks[:]

    def relall():
        free_banks.extend(range(8))

    def rot(lst, name):
        ctr = [0]

        def get():
            i = ctr[0] % len(lst)
            ctr[0] += 1
            return lst[i], (name, i)
        return get

    sqtile = rot(sq_t, "sq")
    qsqtile = rot(qsq_t, "qsq")
    rstile = rot(rs_t, "rs")
    pttile = rot(pt_t, "pt")
    extile = rot(ex_t, "ex")
    rectile = rot(rec_t, "rec")
    rltile = rot(rl_t, "rl")

    def hc(c, t, lo=0, n=SUB):
        return h_sb[:, c * TILE + t * SUB + lo: c * TILE + t * SUB + lo + n]

    def xc(c, t, lo=0, n=SUB):
        return xn_sb[:, c * TILE + t * SUB + lo: c * TILE + t * SUB + lo + n]

    hk = lambda c, t: ("h", c, t)
    xk = lambda c, t: ("xn", c, t)
    pcol = lambda i: par_sb[:, i:i + 1]

    def hidc(j, t):
        return hid_sb[:, j * TILE + t * SUB: j * TILE + (t + 1) * SUB]

    def hidk(j, t):
        return ck(j * 2048 + t * 1024, j * 2048 + (t + 1) * 1024)

    def dc(c, t):
        return d_sb[:, c * TILE + t * SUB: c * TILE + (t + 1) * SUB]

    def dk(c, t):
        return ck(D_OFF + c * 2048 + t * 1024, D_OFF + c * 2048 + (t + 1) * 1024)

    def uk(c, t):
        b = c * 4160
        if t < 0:
            return ck(b, b + 64)
        return ck(b + 64 + t * 2048, b + 64 + (t + 1) * 2048)

    def qc(c, t, is_a):
        if is_a:
            return q67_sb[:, (c - 6) * TILE + t * SUB: (c - 6) * TILE + (t + 1) * SUB]
        return q_sb[:, c * TILE + t * SUB: c * TILE + (t + 1) * SUB]

    def qk(c, t, is_a):
        if is_a:
            b0 = Q67_OFF + (c - 6) * 2048 + t * 1024
        else:
            b0 = c * 2048 + t * 1024
        return ck(b0, b0 + 1024)

    wseq = []
    for l in layers:
        wseq.append(("wmem", l, wmem_d[l], 4096))
    for T in range(n_tiles):
        for l in layers:
            if do_kv and l == kv_layer:
                wseq.append(("wkd", T, wkd_d, 4096))
                wseq.append(("wv", T, wv_d, 2048))
            wseq.append(("win", (T, l, 0), win_d[l, 0], 4096))
            wseq.append(("win", (T, l, 1), win_d[l, 1], 4096))
            if l < N_A:
                wseq.append(("wpool", (T, l, 0), wpool_d[l, 0], 2304))
                wseq.append(("wpool", (T, l, 1), wpool_d[l, 1], 2304))
            wseq.append(("wout", (T, l, 0), wout_d[l, 0], 4096))
            wseq.append(("wout", (T, l, 1), wout_d[l, 1], 4096))
            for hf in range(2):
                for jp in range(4):
                    wseq.append(("wup", (T, l, hf, jp), wup_d[l, hf * 4 + jp], 4096))
                for ip in range(4):
                    wseq.append(("wdn", (T, l, hf, ip), wdn_d[l, hf, ip], 4096))
    w_issued = [0]
    w_used = [0]
    w_released = set()

    def w_pump():
        while w_issued[0] < len(wseq):
            i = w_issued[0]
            if i - NSLOT >= 0 and (i - NSLOT) not in w_released:
                break
            if i >= w_used[0] + NSLOT:
                break
            _, _, ap, n = wseq[i]
            s = i % NSLOT
            P.add("pool", (lambda ap=ap, n=n, s=s: lambda e: e.dma_start(out=ring[s][:, 0:n], in_=ap))(),
                  w=[("slot", s)], dsem=s)
            w_issued[0] += 1

    def wnext(kind, ident):
        i = w_used[0]
        assert wseq[i][0] == kind and wseq[i][1] == ident, (wseq[i][:2], kind, ident)
        w_used[0] += 1
        w_pump()
        assert w_issued[0] > i, "weight piece could not be issued (ring full of unreleased pieces)"
        s = i % NSLOT
        return ring[s], ("slot", s), i

    def wrel(i):
        w_released.add(i)
        w_pump()

    DS_PAR = NSLOT
    DS_X = [NSLOT + 1, NSLOT + 2]
    DS_ST = [NSLOT + 3 + i for i in range(4)]
    N_DSEM = NSLOT + 7
    st_ctr = [0]

    def mm(out, lhsT, rhs, start, stop, r, w, sgc=False):
        if sgc:
            P.add("pe", lambda e: e.matmul(out, lhsT, rhs, start=start, stop=stop, skip_group_check=True), r=r, w=w)
        else:
            P.add("pe", lambda e: e.matmul(out, lhsT, rhs, start=start, stop=stop), r=r, w=w)

    def mm_group(out, pairs, r, w):
        n = len(pairs)

        def fn(e):
            ins = None
            for i, (lt, rh) in enumerate(pairs):
                ins = e.matmul(out, lt, rh, start=(i == 0), stop=(i == n - 1))
            return ins
        P.add("pe", fn, r=r, w=w)

    def act(out, in_, func, r, w, scale=1.0, bias=None):
        if bias is None:
            P.add("act", lambda e: e.activation(out=out, in_=in_, func=func, scale=scale), r=r, w=w)
        else:
            P.add("act", lambda e: e.activation(out=out, in_=in_, func=func, scale=scale, bias=bias), r=r, w=w)

    def tt(out, in0, in1, op, r, w, eng="dve"):
        P.add(eng, lambda e: e.tensor_tensor(out=out, in0=in0, in1=in1, op=op), r=r, w=w)

    def stt(out, in0, scalar, in1, op0, op1, r, w, eng="dve"):
        P.add(eng, lambda e: e.scalar_tensor_tensor(out=out, in0=in0, scalar=scalar, in1=in1, op0=op0, op1=op1),
              r=r, w=w)

    def rstd_from_bank(bank, bkey, n=SUB):
        rs, rsk = rstile()
        act(rs[:, 0:n], bank[:, 0:n], AF.Ln, r=[bkey, "const"], w=[rsk], bias=epsc)
        act(rs[:, 0:n], rs[:, 0:n], AF.Exp, r=[rsk], w=[rsk], scale=-0.5)
        return rs, rsk

    def qnorm_a(bank, bkey, n=SUB):
        sq, sqk = qsqtile()
        act(sq[:, 0:n], bank[:, 0:n], AF.Square, r=[bkey], w=[sqk])
        return sq, sqk

    def qnorm_b(bank, bkey, sq, sqk, gcol, dst, dkeys, n=SUB):
        b2, b2k = newbank()
        mm(b2[:, 0:n], bd_bf, sq[:, 0:n], True, True, r=[sqk, "const"], w=[b2k])
        rs, rsk = rstd_from_bank(b2, b2k, n)
        stt(dst, bank[:, 0:n], gcol, rs[:, 0:n], ALU.mult, ALU.mult, r=[bkey, rsk, "params"], w=dkeys)

    def qnorm(bank, bkey, gcol, dst, dkeys, n=SUB):
        sq, sqk = qnorm_a(bank, bkey, n)
        qnorm_b(bank, bkey, sq, sqk, gcol, dst, dkeys, n)

    def norm_t(gcol0, t):
        bank, bkey = newbank()
        for c in range(8):
            sq, sqk = sqtile()
            act(sq, hc(c, t), AF.Square, r=[hk(c, t)], w=[sqk])
            mm(bank, ones_bf, sq, c == 0, c == 7, r=[sqk, "const"], w=[bkey])
        rs, rsk = rstd_from_bank(bank, bkey)
        for c in range(8):
            stt(xc(c, t), hc(c, t), pcol(gcol0 + c), rs, ALU.mult, ALU.mult,
                r=[hk(c, t), rsk, "params"], w=[xk(c, t)])

    def lagged(items, stage1, stage2, lag):
        pend = []
        for it in items:
            stage1(it)
            pend.append(it)
            if len(pend) > lag:
                stage2(pend.pop(0))
        while pend:
            stage2(pend.pop(0))

    P.add("sp", lambda e: e.dma_start(out=par_sb, in_=par_d), w=["params"], dsem=DS_PAR)
    dist_sb = cview(0, 256, F32)
    mask_sb = cview(1024, 256, F32)
    P.add("sp", lambda e: e.dma_start(out=dist_sb, in_=cst_d[:, 0:256]), w=ck(0, 1024), dsem=DS_PAR)
    P.add("sp", lambda e: e.dma_start(out=mask_sb, in_=cst_d[:, 256:512]), w=ck(1024, 2048), dsem=DS_PAR)
    P.add("sp", lambda e: e.dma_start(out=invcnt, in_=cst_d[:, 512:576]), w=["invcnt"], dsem=DS_PAR)
    P.add("dve", lambda e: e.memset(ones_bf, 1.0 / 1024.0), w=["const"])
    P.add("dve", lambda e: e.memset(bd_bf, 0.0), w=["const"])
    P.add("dve", lambda e: e.memset(bd_bf[0:64, 0:64], 1.0 / 64.0), w=["const"])
    P.add("dve", lambda e: e.memset(bd_bf[64:128, 64:128], 1.0 / 64.0), w=["const"])
    P.add("dve", lambda e: e.memset(ones64, 1.0), w=["const"])
    P.add("dve", lambda e: e.memset(epsc, EPS), w=["const"])
    if has_b:
        act(esink, par_sb[:, PC_SINK:PC_SINK + 12], AF.Exp, r=["params"], w=["esink"])
        for hh in range(12):
            eh_ = etab[:, hh * 256:(hh + 1) * 256]
            act(eh_, dist_sb, AF.Exp, r=ck(0, 1024), w=[("etab", hh)], scale=-SLOPES[hh])
            tt(eh_, eh_, mask_sb, ALU.mult, r=[("etab", hh)] + ck(1024, 2048), w=[("etab", hh)])

    MX_OFF = 2048
    memx = cview(MX_OFF, 8 * 256, F32)
    MN_OFF = MX_OFF + 8192
    memn = cview(MN_OFF, 8 * 256, BF16)
    MR_OFF = MN_OFF + 4096
    memrs = cview(MR_OFF, 256, F32)
    P.add("sp", lambda e: e.dma_start(out=memx.rearrange("p (c n) -> p c n", c=8),
                                      in_=memT_d.rearrange("(c p) n -> p c n", p=128)),
          w=ck(MX_OFF, MX_OFF + 8192), dsem=DS_PAR)
    bank, bkey = newbank()
    for c in range(8):
        sq, sqk = sqtile()
        act(sq[:, 0:256], memx[:, c * 256:(c + 1) * 256], AF.Square, r=ck(MX_OFF, MX_OFF + 8192), w=[sqk])
        mm(bank[:, 0:256], ones_bf, sq[:, 0:256], c == 0, c == 7, r=[sqk, "const"], w=[bkey])
    act(memrs, bank[:, 0:256], AF.Ln, r=[bkey, "const"], w=ck(MR_OFF, MR_OFF + 1024), bias=epsc)
    act(memrs, memrs, AF.Exp, r=ck(MR_OFF, MR_OFF + 1024), w=ck(MR_OFF, MR_OFF + 1024), scale=-0.5)
    for l in layers:
        for c in range(8):
            stt(memn[:, c * 256:(c + 1) * 256], memx[:, c * 256:(c + 1) * 256], pcol(PC_MEMN + 8 * l + c), memrs,
                ALU.mult, ALU.mult, r=ck(MX_OFF, MX_OFF + 8192) + ck(MR_OFF, MR_OFF + 1024) + ["params"],
                w=ck(MN_OFF, MN_OFF + 4096))
        ws, wk, wi = wnext("wmem", l)
        for cmi in range(2):
            bank, bkey = newbank()
            mm_group(bank[:, 0:256],
                     [(ws[:, kc * 512 + cmi * 128: kc * 512 + (cmi + 1) * 128], memn[:, kc * 256:(kc + 1) * 256])
                      for kc in range(8)], r=[wk] + ck(MN_OFF, MN_OFF + 4096), w=[bkey])
            o_ = (l * 2 + cmi) * 256
            qnorm(bank, bkey, pcol(PC_MKN + l), mk_sb[:, o_:o_ + 256], [("mk", l, cmi)], n=256)
        for mc in range(2):
            bank, bkey = newbank()
            mm_group(bank[:, 0:256],
                     [(memn[:, kc * 256 + mc * 128: kc * 256 + (mc + 1) * 128], ws[:, kc * 512 + 256: kc * 512 + 512])
                      for kc in range(8)], r=[wk] + ck(MN_OFF, MN_OFF + 4096), w=[bkey])
            o_ = (l * 2 + mc) * 256
            P.add("act", (lambda o_=o_, bank=bank: lambda e: e.copy(out=mv_sb[:, o_:o_ + 256], in_=bank[:, 0:256]))(),
                  r=[bkey], w=[("mv", l, mc)])
        wrel(wi)

    def in_t(T, l, t, wsl, hook=None):
        is_a = l < N_A
        st = {}

        def s1(c):
            ws, wk, _ = wsl[c // 4]
            bank, bkey = newbank()
            mm_group(bank, [(ws[:, kc * 512 + (c % 4) * 128: kc * 512 + (c % 4 + 1) * 128], xc(kc, t))
                            for kc in range(8)], r=[wk] + [xk(kc, t) for kc in range(8)], w=[bkey])
            if is_a and c < 6:
                dst = u_sb[:, c * 1040 + 16 + t * SUB: c * 1040 + 16 + (t + 1) * SUB]
                P.add("act", (lambda dst=dst, bank=bank: lambda e: e.copy(out=dst, in_=bank))(), r=[bkey], w=uk(c, t))
                st[c] = None
            else:
                sq, sqk = qnorm_a(bank, bkey)
                st[c] = (bank, bkey, sq, sqk)
            if hook is not None and c == 1:
                hook()

        def s2(c):
            if st[c] is None:
                return
            bank, bkey, sq, sqk = st[c]
            if c >= 6:
                qnorm_b(bank, bkey, sq, sqk, pcol(PC_MQN + l), qc(c, t, is_a), qk(c, t, is_a))
            else:
                qnorm_b(bank, bkey, sq, sqk, pcol(PC_QN + (l - N_A)), qc(c, t, False), qk(c, t, False))

        lagged(list(range(8)), s1, s2, 1)

    POOL_PIECES = {0: [(0, 0, 128), (1, 0, 64)], 1: [(1, 64, 128), (2, 0, 128)],
                   2: [(3, 0, 128), (4, 0, 64)], 3: [(4, 64, 128), (5, 0, 128)]}
    POOL_NZ = {0: [0, 1], 1: [0, 1, 2], 2: [1, 2], 3: [3, 4], 4: [3, 4, 5], 5: [4, 5]}
    u3 = u_sb.rearrange("p (c n) -> p c n", c=6)

    def pool_halo(T, l):
        cr3 = carry[:, l * 96:(l + 1) * 96].rearrange("p (c n) -> p c n", c=6)
        hkeys = [k for c in range(6) for k in uk(c, -1)]
        if T == 0:
            P.add("dve", lambda e: e.memset(u3[:, :, 0:16], 0.0), w=hkeys)
        else:
            P.add("dve", lambda e: e.tensor_copy(out=u3[:, :, 0:16], in_=cr3), r=[("carry", l)], w=hkeys)

    def pool_dve(T, l, t, eng="dve"):
        sabx = sab if eng == "dve" else sab2
        sname = "sab" if eng == "dve" else "sab2"
        for g in range(4):
            W = POOL_W[g]
            for (c, plo, phi) in POOL_PIECES[g]:
                base = c * 1040 + t * SUB
                src = u_sb[plo:phi, base: base + 528]
                srck = uk(c, t) + (uk(c, t - 1) if t > 0 else uk(c, -1))
                cur = src
                curk = srck
                k = 1
                si = 0
                while k < W:
                    lo = 16 - (W - 2 * k)
                    o = sabx[si][plo:phi, :]
                    ok = [(sname, si)]
                    tt(o[:, lo:528], cur[:, lo:528], cur[:, lo - k:528 - k], ALU.add, r=curk, w=ok, eng=eng)
                    cur, curk = o, ok
                    si ^= 1
                    k *= 2
                dst = d_sb[plo:phi, c * TILE + t * SUB: c * TILE + (t + 1) * SUB]
                stt(dst, cur[:, 16:528], 1.0 / W, src[:, 16:528], ALU.mult, ALU.subtract,
                    r=curk + srck, w=dk(c, t), eng="dve")
                if T == 0 and t == 0:
                    ic = invcnt[plo:phi, 16 * g:16 * (g + 1)]
                    tt(cur[:, 16:32], cur[:, 16:32], ic, ALU.mult, r=curk + ["invcnt"], w=curk, eng=eng)
                    tt(dst[:, 0:16], cur[:, 16:32], src[:, 16:32], ALU.subtract, r=curk + srck, w=dk(c, t), eng=eng)
        if t == 1:
            cr3 = carry[:, l * 96:(l + 1) * 96].rearrange("p (c n) -> p c n", c=6)
            P.add(eng, lambda e: e.tensor_copy(out=cr3, in_=u3[:, :, 1024:1040]),
                  r=[k for c in range(6) for k in uk(c, 1)], w=[("carry", l)])

    def pool_mm(T, l, t, wps):
        for oc in range(6):
            ws, wk, _ = wps[oc // 3]
            bank, bkey = newbank()
            rk = [wk]
            for kc in POOL_NZ[oc]:
                rk += dk(kc, t)
            mm_group(bank, [(ws[:, kc * 384 + (oc % 3) * 128: kc * 384 + (oc % 3 + 1) * 128], dc(kc, t))
                            for kc in POOL_NZ[oc]], r=rk, w=[bkey])
            act(xc(oc, t), bank, AF.Copy, r=[bkey, "params"], w=[xk(oc, t)], scale=pcol(PC_PSC + 6 * l + oc))

    def mem_t(T, l, t):
        is_a = l < N_A
        for cmi in range(2):
            ob, obk, obi = holdbank()
            db, dbk, dbi = holdbank()
            st = {}

            def s1(it, cmi=cmi):
                hh, mc = it
                po = 64 * hh
                sb, sbk = newbank()
                o_ = (l * 2 + cmi) * 256 + mc * 128
                qa = qc(6 + cmi, t, is_a)
                mm(sb, mk_sb[po:po + 64, o_:o_ + 128], qa[po:po + 64, :], True, True,
                   r=[("mk", l, cmi)] + qk(6 + cmi, t, is_a), w=[sbk])
                pt, ptk = pttile()
                act(pt, sb, AF.Exp, r=[sbk], w=[ptk], scale=0.125)
                st[it] = (pt, ptk)

            def s2(it, cmi=cmi, ob=ob, obk=obk, db=db, dbk=dbk):
                hh, mc = it
                po = 64 * hh
                hm = 2 * cmi + hh
                pt, ptk = st[it]
                o_ = (l * 2 + mc) * 256 + hm * 64
                mm(ob[po:po + 64, :], mv_sb[:, o_:o_ + 64], pt, mc == 0, mc == 1, r=[ptk, ("mv", l, mc)], w=[obk])
                mm(db[po:po + 64, :], ones64, pt, mc == 0, mc == 1, r=[ptk, "const"], w=[dbk])

            lagged([(hh, mc) for mc in range(2) for hh in range(2)], s1, s2, 2)
            rec, reck = rectile()
            act(rec, db, AF.Ln, r=[dbk], w=[reck])
            act(rec, rec, AF.Exp, r=[reck], w=[reck], scale=-1.0)
            tt(xc(6 + cmi, t), ob, rec, ALU.mult, r=[obk, reck], w=[xk(6 + cmi, t)])
            relbank(obi)
            relbank(dbi)

    def swa_phase(T, l):
        jl = l - N_A
        holdall()
        hb_ = [(banks[b], ("B", b), b) for b in range(4)]
        sets = [(hb_[0], hb_[1]), (hb_[2], hb_[3])]
        spair = [(ps_all[:, 2048:3072], [("B", 4), ("B", 5)]), (ps_all[:, 3072:4096], [("B", 6), ("B", 7)])]
        sp_ctr = [0]
        etab3 = etab.rearrange("p (h n) -> p h n", h=12)
        items = []
        first_j = {}
        last_j = {}
        for c in range(6):
            for t in range(2):
                js = [j for j in range(4 * t - 1, 4 * t + 4) if not (T == 0 and j == -1)]
                for j in js:
                    items.append((c, t, j))
                first_j[(c, t)] = js[0]
                last_j[(c, t)] = js[-1]
        st = {}
        keep = {}

        def s1(it):
            c, t, j = it
            if t == 1 and j == 3:
                st[it] = keep[c]
                return
            b0, b1 = max(j, 0), min(j + 1, 7)
            wdt = (b1 - b0 + 1) * 128
            tc0 = 128 if j == -1 else 0
            sb, sbks = spair[sp_ctr[0] % 2]
            sp_ctr[0] += 1
            for hh in range(2):
                h_ = 2 * c + hh
                kvh = h_ // 3
                po = 64 * hh
                rk = [("K", kvh, j + 1)]
                for b in range(b0, b1 + 1):
                    rk += qk(c, b // 4, False)
                mm(sb[:, hh * 512: hh * 512 + wdt],
                   kd_sb[po:po + 64, kvh * 1152 + (j + 1) * 128: kvh * 1152 + (j + 2) * 128],
                   q_sb[po:po + 64, c * TILE + b0 * 128: c * TILE + (b1 + 1) * 128], True, True, r=rk,
                   w=[sbks[hh]])
            ex, exk = extile()
            if t == 0 and j == 3:
                pt, ptk = keep_t[c % 2], ("keep", c % 2)
                keep[c] = (pt, ptk)
            else:
                pt, ptk = pttile()
            sbv = sb.rearrange("p (h n) -> p h n", h=2)[:, :, 0:wdt]
            if wdt == 256:
                exv, ptv = ex, pt
                exv_a = ex.rearrange("p (h n) -> p h n", h=2)
            else:
                exv = ex.rearrange("p (h n) -> p h n", h=2)[:, :, 0:wdt]
                ptv = pt.rearrange("p (h n) -> p h n", h=2)[:, :, 0:wdt]
                exv_a = exv
            act(exv_a, sbv, AF.Exp, r=sbks, w=[exk], scale=0.125)
            tt(ptv, exv, etab3[:, 2 * c:2 * c + 2, tc0:tc0 + wdt] if wdt != 256 else
               etab[:, 2 * c * 256:(2 * c + 2) * 256], ALU.mult,
               r=[exk, ("etab", 2 * c), ("etab", 2 * c + 1)], w=[ptk])
            st[it] = (pt, ptk)

        def s2(it):
            c, t, j = it
            (ob, obk, _), (db, dbk, _) = sets[t]
            pt, ptk = st[it]
            vk = ("V", j + 1)
            if j == 4 * t - 1:
                pc0, pw, oc0 = (0 if j == -1 else 128), 128, 0
            elif j == 4 * t + 3:
                pc0, pw, oc0 = 0, 128, 384
            else:
                pc0, pw, oc0 = 0, 256, (j - 4 * t) * 128
            first = (j == first_j[(c, t)])
            for hh in range(2):
                kvh = (2 * c + hh) // 3
                po = 64 * hh
                vap = v_sb[:, (j + 1) * 256 + kvh * 64: (j + 1) * 256 + (kvh + 1) * 64]
                mm(ob[po:po + 64, oc0:oc0 + pw], vap, pt[:, hh * 256 + pc0: hh * 256 + pc0 + pw], first, True,
                   r=[ptk, vk], w=[obk], sgc=True)
            for hh in range(2):
                po = 64 * hh
                mm(db[po:po + 64, oc0:oc0 + pw], ones64, pt[:, hh * 256 + pc0: hh * 256 + pc0 + pw], first, True,
                   r=[ptk, "const"], w=[dbk], sgc=True)
            if j == last_j[(c, t)]:
                rec, reck = rectile()
                act(rec, db, AF.Ln, r=[dbk, "esink"], w=[reck], bias=esink[:, jl * 6 + c: jl * 6 + c + 1])
                act(rec, rec, AF.Exp, r=[reck], w=[reck], scale=-1.0)
                tt(xc(c, t), ob, rec, ALU.mult, r=[obk, reck], w=[xk(c, t)])

        lagged(items, s1, s2, 2)
        relall()

    def out_phase(T, l, hook=None):
        w0 = wnext("wout", (T, l, 0))
        w1 = wnext("wout", (T, l, 1))
        wsl = [w0, w1]
        for t in range(2):
            for c in range(8):
                ws, wk, _ = wsl[c // 4]
                bank, bkey = newbank()
                mm_group(bank, [(ws[:, kc * 512 + (c % 4) * 128: kc * 512 + (c % 4 + 1) * 128], xc(kc, t))
                                for kc in range(8)], r=[wk] + [xk(kc, t) for kc in range(8)], w=[bkey])
                tt(hc(c, t), bank, hc(c, t), ALU.add, r=[bkey, hk(c, t)], w=[hk(c, t)])
                if hook is not None and t == 1 and c == 1:
                    hook()
        wrel(w0[2])
        wrel(w1[2])

    def up_group(ws, wk, jj, j, t):
        bank, bkey = newbank()
        mm_group(bank, [(ws[:, kc * 512 + jj * 128: kc * 512 + (jj + 1) * 128], xc(kc, t))
                        for kc in range(8)], r=[wk] + [xk(kc, t) for kc in range(8)], w=[bkey])
        rl, rlk = rltile()
        act(rl, bank, AF.Relu, r=[bkey], w=[rlk])
        tt(hidc(j, t), rl, rl, ALU.mult, r=[rlk], w=hidk(j, t))

    def down_group(T, ws, wk, ii, i, t, store):
        bank, bkey = newbank()
        rk = [wk]
        for kc in range(16):
            rk += hidk(kc, t)
        mm_group(bank, [(ws[:, kc * 256 + ii * 128: kc * 256 + (ii + 1) * 128], hidc(kc, t))
                        for kc in range(16)], r=rk, w=[bkey])
        tt(hc(i, t), bank, hc(i, t), ALU.add, r=[bkey, hk(i, t)], w=[hk(i, t)])
        if store:
            ds = DS_ST[st_ctr[0] % 4]
            st_ctr[0] += 1
            dst = yT_d[i * 128:(i + 1) * 128, T * TILE + t * SUB: T * TILE + (t + 1) * SUB]
            P.add("sp", (lambda dst=dst, src=hc(i, t): lambda e: e.dma_start(out=dst, in_=src))(),
                  r=[hk(i, t)], w=[("y", T, i, t)], dsem=ds)

    def mlp_phase(T, l, last, norm2_t1, next_norm_t0, after_t0=None):
        for hf in range(2):
            for pair in range(2):
                wp = [wnext("wup", (T, l, hf, 2 * pair + q)) for q in range(2)]
                for t in range(2):
                    for q in range(2):
                        ws, wk, _ = wp[q]
                        for jj in range(4):
                            up_group(ws, wk, jj, (2 * pair + q) * 4 + jj, t)
                            if hf == 0 and pair == 0 and t == 0 and q == 0 and jj == 1 and norm2_t1 is not None:
                                norm2_t1()
                wrel(wp[0][2])
                wrel(wp[1][2])
            if hf == 0:
                for ip in range(4):
                    ws, wk, wi = wnext("wdn", (T, l, hf, ip))
                    for ii in range(2):
                        for t in range(2):
                            down_group(T, ws, wk, ii, ip * 2 + ii, t, False)
                    wrel(wi)
            else:
                wd = [wnext("wdn", (T, l, hf, ip)) for ip in range(4)]
                for t in range(2):
                    if t == 1 and after_t0 is not None:
                        after_t0()
                    for ip in range(4):
                        ws, wk, _ = wd[ip]
                        for ii in range(2):
                            down_group(T, ws, wk, ii, ip * 2 + ii, t, last)
                            if t == 1 and ip == 0 and ii == 1 and next_norm_t0 is not None:
                                next_norm_t0()
                for x_ in wd:
                    wrel(x_[2])

    def kv_phase(T):
        if T > 0:
            k3 = kd_sb.rearrange("p (k n) -> p k n", k=4)
            P.add("dve", lambda e: e.tensor_copy(out=k3[:, :, 0:128], in_=k3[:, :, 1024:1152]),
                  r=[("K", kvh, 8) for kvh in range(4)], w=[("K", kvh, 0) for kvh in range(4)])
            P.add("dve", lambda e: e.tensor_copy(out=v_sb[:, 0:256], in_=v_sb[:, 8 * 256:9 * 256]),
                  r=[("V", 8)], w=[("V", 0)])
        ws, wk, wi = wnext("wkd", T)
        for t in range(2):
            st = {}

            def s1(kvh, t=t):
                bank, bkey = newbank()
                mm_group(bank, [(ws[:, kc * 512 + kvh * 128: kc * 512 + (kvh + 1) * 128], xc(kc, t))
                                for kc in range(8)], r=[wk] + [xk(kc, t) for kc in range(8)], w=[bkey])
                sq, sqk = qnorm_a(bank, bkey)
                st[kvh] = (bank, bkey, sq, sqk)
                if t == 0 and kvh == 1:
                    norm_t(PC_KVN, 1)

            def s2(kvh, t=t):
                bank, bkey, sq, sqk = st[kvh]
                dst = kd_sb[:, kvh * 1152 + 128 + t * SUB: kvh * 1152 + 128 + (t + 1) * SUB]
                qnorm_b(bank, bkey, sq, sqk, pcol(PC_KN), dst, [("K", kvh, 1 + 4 * t + b) for b in range(4)])

            lagged(list(range(4)), s1, s2, 1)
        wrel(wi)
        ws, wk, wi = wnext("wv", T)
        for n in range(8):
            t = n // 4
            bank, bkey = newbank()
            mm_group(bank[:, 0:256], [(xc(kc, t, (n % 4) * 128, 128), ws[:, kc * 256:(kc + 1) * 256])
                                      for kc in range(8)], r=[wk] + [xk(kc, t) for kc in range(8)], w=[bkey])
            P.add("act", (lambda n=n, bank=bank: lambda e: e.copy(out=v_sb[:, (n + 1) * 256:(n + 2) * 256],
                                                                     in_=bank[:, 0:256]))(),
                  r=[bkey], w=[("V", n + 1)])
        wrel(wi)

    h3 = h_sb.rearrange("p (c n) -> p c n", c=8)
    x3 = xT_d.rearrange("(c p) n -> p c n", p=128)

    def first_norm_col(l):
        return PC_KVN if (do_kv and l == kv_layer) else PC_NMIX + 8 * l

    def xload(T, t):
        src = x3[:, :, T * TILE + t * SUB: T * TILE + (t + 1) * SUB]
        dst = h3[:, :, t * SUB:(t + 1) * SUB]
        P.add("sp", (lambda dst=dst, src=src: lambda e: e.dma_start(out=dst, in_=src))(),
              w=[hk(c, t) for c in range(8)], dsem=DS_X[t])

    for T in range(n_tiles):
        if T == 0:
            xload(T, 0)
            xload(T, 1)
            norm_t(first_norm_col(layers[0]), 0)
        else:
            xload(T, 1)
        for li, l in enumerate(layers):
            is_a = l < N_A
            if do_kv and l == kv_layer:
                kv_phase(T)
                norm_t(PC_NMIX + 8 * l, 0)
            w0 = wnext("win", (T, l, 0))
            w1 = wnext("win", (T, l, 1))
            wsl = [w0, w1]
            in_t(T, l, 0, wsl, hook=(lambda l=l: norm_t(PC_NMIX + 8 * l, 1)))
            if is_a:
                pool_halo(T, l)
                pool_dve(T, l, 0)
            in_t(T, l, 1, wsl)
            wrel(w0[2])
            wrel(w1[2])
            if is_a:
                wps = [wnext("wpool", (T, l, 0)), wnext("wpool", (T, l, 1))]
                mem_t(T, l, 0)
                pool_mm(T, l, 0, wps)
                pool_dve(T, l, 1, eng=POOL_ENG_T1)
                mem_t(T, l, 1)
                pool_mm(T, l, 1, wps)
                wrel(wps[0][2])
                wrel(wps[1][2])
            else:
                mem_t(T, l, 0)
                swa_phase(T, l)
                mem_t(T, l, 1)
            out_phase(T, l, hook=(lambda l=l: norm_t(PC_NMLP + 8 * l, 0)))
            last = (l == layers[-1])
            nxt = None
            aft = None
            if not last:
                nxt = (lambda l=l: norm_t(first_norm_col(l + 1), 0))
            elif T + 1 < n_tiles:
                aft = (lambda T=T: xload(T + 1, 0))
                nxt = (lambda: norm_t(first_norm_col(layers[0]), 0))
            mlp_phase(T, l, last, (lambda l=l: norm_t(PC_NMLP + 8 * l, 1)), nxt, aft)
    P.add("sp", None, r=[("y", T, i, t) for T in range(n_tiles) for i in range(8) for t in range(2)])
    assert w_used[0] == len(wseq)

    sem_cms = {e: nc.semaphore("s_" + e) for e in ENGS}
    esems = {e: c_.__enter__() for e, c_ in sem_cms.items()}
    dsem_cms = [nc.semaphore(f"d{i}") for i in range(N_DSEM)]
    dsems = [c_.__enter__() for c_ in dsem_cms]
    with nc.Block() as block:
        P.emit(block, esems, dsems)
    for c_ in dsem_cms[::-1]:
        c_.__exit__(None, None, None)
    for c_ in list(sem_cms.values())[::-1]:
        c_.__exit__(None, None, None)
    for c_ in pcms[::-1]:
        c_.__exit__(None, None, None)
    cm.__exit__(None, None, None)
    return nc


_NC_CACHE = {}


def _get_nc(l0, l1):
    key = (l0, l1)
    if key not in _NC_CACHE:
        _NC_CACHE[key] = build(l0, l1)
    return _NC_CACHE[key]


LAUNCHES = [(0, 4)]


def kernel(**inputs):
    x = np.asarray(inputs["x"], dtype=np.float32)
    mem = np.asarray(inputs["mem"], dtype=np.float32)
    B = x.shape[0]
    sh = _prep_shared(inputs)
    cur = [np.ascontiguousarray(x[b].T) for b in range(B)]
    memT = [np.ascontiguousarray(mem[b].T) for b in range(B)]
    for (l0, l1) in LAUNCHES:
        nc = _get_nc(l0, l1)
        in_maps = []
        for b in range(B):
            m = dict(sh)
            m["xT"] = cur[b]
            m["memT"] = memT[b]
            in_maps.append(m)
        res = run_bass_kernel_spmd(nc, in_maps, core_ids=list(range(B)))
        cur = [np.asarray(res.results[b]["yT"]) for b in range(B)]
    out = np.stack([np.ascontiguousarray(cur[b].T) for b in range(B)]).astype(np.float32)
    return out
```

```python
import math
import numpy as np
import concourse.bass as bass
import concourse.mybir as mybir
from concourse.bass_utils import run_bass_kernel_spmd

F32 = mybir.dt.float32
BF16 = mybir.dt.bfloat16
ALU = mybir.AluOpType
AF = mybir.ActivationFunctionType

ENGS = ("pe", "act", "dve", "pool", "sp")

D = 1024
SEQ = 4096
NB = 8
TILE = 1024
NT = SEQ // TILE
SUB = 512
MEM = 256
EPS = 1e-6
N_A = 2
SLOPES = [2.0 ** (-8.0 * (i + 1) / 12.0) for i in range(12)]
POOL_W = (2, 4, 8, 16)
NSLOT = 6
POOL_ENG_T1 = "dve"
SLOT_ELEMS = 4096

PC_NMIX = 0
PC_NMLP = 32
PC_MEMN = 64
PC_KVN = 96
PC_PSC = 104
PC_QN = 116
PC_KN = 118
PC_MQN = 119
PC_MKN = 123
PC_SINK = 127
NPAR = 144


class Op:
    __slots__ = ("eng", "fn", "deps", "sig", "cnt", "dsem")

    def __init__(self, eng, fn, dsem):
        self.eng = eng
        self.fn = fn
        self.dsem = dsem
        self.deps = ()
        self.sig = False
        self.cnt = 0


class Prog:
    def __init__(self):
        self.ops = {e: [] for e in ENGS}
        self.lastw = {}
        self.readers = {}
        self.dsem_last = {}

    def add(self, eng, fn, r=(), w=(), dsem=None):
        op = Op(eng, fn, dsem)
        deps = set()
        for k in r:
            lw = self.lastw.get(k)
            if lw is not None:
                deps.add(lw)
        for k in w:
            lw = self.lastw.get(k)
            if lw is not None:
                deps.add(lw)
            rs = self.readers.get(k)
            if rs:
                deps.update(rs)
        for k in w:
            self.lastw[k] = op
            self.readers[k] = []
        for k in r:
            if k not in w:
                self.readers.setdefault(k, []).append(op)
        if dsem is not None:
            prev = self.dsem_last.get(dsem)
            if prev is not None:
                deps.add(prev)
            self.dsem_last[dsem] = op
        deps.discard(op)
        op.deps = deps
        for d in deps:
            if d.dsem is None and not (d.eng == "pe" and eng == "pe"):
                d.sig = True
        self.ops[eng].append(op)
        return op

    def emit(self, block, esems, dsems):
        for e in ENGS:
            n = 0
            for op in self.ops[e]:
                if op.dsem is None and op.sig:
                    n += 1
                    op.cnt = n
        dcount = {}
        for e in ENGS:
            for op in self.ops[e]:
                if op.dsem is not None:
                    dcount[op.dsem] = dcount.get(op.dsem, 0) + 16
                    op.cnt = dcount[op.dsem]

        def run(e, eh):
            known = {}
            for op in self.ops[e]:
                waits = {}
                for d in op.deps:
                    if d.dsem is not None:
                        key = ("d", d.dsem)
                    else:
                        if d.eng == "pe" and e == "pe":
                            continue
                        key = ("e", d.eng)
                    if d.cnt > waits.get(key, 0):
                        waits[key] = d.cnt
                for key, v in waits.items():
                    if known.get(key, 0) >= v:
                        continue
                    known[key] = v
                    sem = dsems[key[1]] if key[0] == "d" else esems[key[1]]
                    eh.wait_ge(sem, v)
                ins = op.fn(eh) if op.fn is not None else None
                if ins is not None:
                    if op.dsem is not None:
                        ins.then_inc(dsems[op.dsem], 16)
                    elif op.sig:
                        ins.then_inc(esems[e], 1)
                else:
                    assert op.dsem is None and not op.sig

        @block.tensor
        def _(eh):
            run("pe", eh)

        @block.scalar
        def _(eh):
            run("act", eh)

        @block.vector
        def _(eh):
            run("dve", eh)

        @block.gpsimd
        def _(eh):
            run("pool", eh)

        @block.sync
        def _(eh):
            run("sp", eh)


def _kc_layout(w, ncols):
    K, N = w.shape
    kcn = K // 128
    a = w.reshape(kcn, 128, N // ncols, ncols).transpose(2, 1, 0, 3)
    return np.ascontiguousarray(a).reshape(N // ncols, 128, kcn * ncols)


def _prep_shared(inp):
    f = lambda a: np.asarray(a, dtype=np.float32)
    w_in, w_out, w_up, w_down = f(inp["w_in"]), f(inp["w_out"]), f(inp["w_up"]), f(inp["w_down"])
    pool_w, w_kv, w_mem_kv = f(inp["pool_w"]), f(inp["w_kv"]), f(inp["w_mem_kv"])
    sh = {}
    sh["win"] = np.stack([_kc_layout(w_in[l], 512) for l in range(4)])
    sh["wout"] = np.stack([_kc_layout(w_out[l], 512) for l in range(4)])
    sh["wup"] = np.stack([_kc_layout(w_up[l], 512) for l in range(4)])
    sh["wdn"] = np.stack([np.stack([_kc_layout(w_down[l][hf * 2048:(hf + 1) * 2048], 256) for hf in range(2)])
                          for l in range(4)])
    wp = []
    for l in range(N_A):
        bd = np.zeros((768, 768), np.float32)
        for g in range(4):
            bd[192 * g:192 * (g + 1), 192 * g:192 * (g + 1)] = pool_w[l, g]
        wp.append(_kc_layout(bd, 384))
    sh["wpool"] = np.stack(wp)
    kpart = w_kv[:, :256].reshape(1024, 4, 1, 64)
    kdup = np.concatenate([kpart, kpart], axis=2).reshape(1024, 512)
    sh["wkd"] = _kc_layout(kdup, 512)[0]
    sh["wv"] = _kc_layout(np.ascontiguousarray(w_kv[:, 256:]), 256)[0]
    sh["wmem"] = np.stack([_kc_layout(w_mem_kv[l], 512)[0] for l in range(4)])

    par = np.zeros((128, NPAR), np.float32)
    p = np.arange(128)

    def chunked(vec, col0):
        v = vec.reshape(-1, 128)
        for c in range(v.shape[0]):
            par[:, col0 + c] = v[c]

    for l in range(4):
        chunked(f(inp["norm_mix"])[l], PC_NMIX + 8 * l)
        chunked(f(inp["norm_mlp"])[l], PC_NMLP + 8 * l)
        chunked(f(inp["mem_norm"])[l], PC_MEMN + 8 * l)
        par[:, PC_MQN + l] = f(inp["mem_q_norm"])[l][p % 64]
        par[:, PC_MKN + l] = f(inp["mem_k_norm"])[l][p % 64]
    chunked(f(inp["kv_norm"]), PC_KVN)
    for l in range(N_A):
        chunked(f(inp["pool_scale"])[l], PC_PSC + 6 * l)
    for j in range(2):
        par[:, PC_QN + j] = f(inp["q_norm"])[j][p % 64]
        for c in range(6):
            par[:, PC_SINK + 6 * j + c] = f(inp["sinks"])[j][2 * c + p // 64]
    par[:, PC_KN] = f(inp["k_norm"])[p % 64]
    sh["params"] = par

    cst = np.zeros((128, 576), np.float32)
    r = np.arange(128)[:, None]
    c = np.arange(256)[None, :]
    dist = (c - r).astype(np.float32)
    cst[:, 0:256] = dist
    cst[:, 256:512] = ((dist >= 0) & (dist < 128)).astype(np.float32)
    for g, w in enumerate(POOL_W):
        cst[:, 512 + 16 * g:512 + 16 * (g + 1)] = 1.0 / np.minimum(np.arange(16) + 1, w).astype(np.float32)
    sh["consts"] = cst
    return sh


def build(l0, l1, n_tiles=NT):
    nc = bass.Bass("TRN2", target_bir_lowering=False)
    dr = lambda name, shape, kind="ExternalInput": nc.dram_tensor(name, shape, F32, kind=kind).ap()
    xT_d = dr("xT", [D, SEQ])
    memT_d = dr("memT", [D, MEM])
    par_d = dr("params", [128, NPAR])
    cst_d = dr("consts", [128, 576])
    win_d = dr("win", [4, 2, 128, 4096])
    wout_d = dr("wout", [4, 2, 128, 4096])
    wup_d = dr("wup", [4, 8, 128, 4096])
    wdn_d = dr("wdn", [4, 2, 4, 128, 4096])
    wpool_d = dr("wpool", [2, 2, 128, 2304])
    wkd_d = dr("wkd", [128, 4096])
    wv_d = dr("wv", [128, 2048])
    wmem_d = dr("wmem", [4, 128, 4096])
    yT_d = dr("yT", [D, SEQ], kind="ExternalOutput")

    P = Prog()
    layers = list(range(l0, l1))
    do_kv = (l0 <= 2 < l1) or (l0 >= 2)
    has_b = any(l >= N_A for l in layers)
    kv_layer = max(l0, 2)

    ARENA_BYTES = 212736
    cm = nc.sbuf_tensor("arena", [128, ARENA_BYTES // 4], F32)
    arena = cm.__enter__()
    off = [0]

    def alloc(n, dt):
        nb = n * (2 if dt == BF16 else 4)
        nb4 = (nb + 3) // 4
        o = off[0]
        off[0] += nb4
        assert off[0] * 4 <= ARENA_BYTES, f"SBUF arena overflow {off[0] * 4}"
        v = arena[:, o:o + nb4]
        return v.bitcast(dt) if dt != F32 else v

    h_sb = alloc(8 * TILE, F32)
    xn_sb = alloc(8 * TILE, BF16)
    ring = [alloc(SLOT_ELEMS, BF16) for _ in range(NSLOT)]
    mk_sb = alloc(4 * 2 * 256, BF16)
    mv_sb = alloc(4 * 2 * 256, BF16)
    kd_sb = alloc(4 * 1152, BF16)
    v_sb = alloc(9 * 256, BF16)
    etab = alloc(12 * 256, F32)
    par_sb = alloc(NPAR, F32)
    esink = alloc(12, F32)
    invcnt = alloc(64, F32)
    epsc = alloc(1, F32)
    _pad = alloc(1, F32)
    ones_bf = alloc(128, BF16)
    bd_bf = alloc(128, BF16)
    ones64 = alloc(64, BF16)
    carry = alloc(2 * 6 * 16, F32)
    sq_t = [alloc(512, BF16) for _ in range(2)]
    qsq_t = [alloc(512, BF16) for _ in range(2)]
    rs_t = [alloc(512, F32) for _ in range(2)]
    pt_t = [alloc(512, BF16) for _ in range(4)]
    ex_t = [alloc(512, F32) for _ in range(2)]
    keep_t = [alloc(512, BF16) for _ in range(2)]
    rec_t = [alloc(512, F32) for _ in range(2)]
    rl_t = [alloc(512, F32) for _ in range(2)]
    sab = [alloc(528, F32) for _ in range(2)]
    sab2 = [alloc(528, F32) for _ in range(2)]
    C_BYTES = 41984
    c_off = off[0]
    off[0] += C_BYTES // 4
    assert off[0] * 4 <= ARENA_BYTES, f"SBUF arena overflow {off[0] * 4}"

    def cview(byte_off, n, dt):
        nb = n * (2 if dt == BF16 else 4)
        v = arena[:, c_off + byte_off // 4: c_off + (byte_off + nb) // 4]
        return v.bitcast(dt) if dt != F32 else v

    def ck(byte_lo, byte_hi):
        return [("C", s) for s in range(byte_lo // 64, (byte_hi - 1) // 64 + 1)]

    hid_sb = cview(0, 16 * TILE, BF16)
    u_sb = cview(0, 6 * 1040, F32)
    D_OFF = 25600
    d_sb = cview(D_OFF, 6 * TILE, BF16)
    Q67_OFF = D_OFF + 12288
    q67_sb = cview(Q67_OFF, 2 * TILE, BF16)
    q_sb = cview(0, 8 * TILE, BF16)

    pcms = [nc.psum_tensor("ps_all", [128, 4096], F32)]
    ps_all = pcms[0].__enter__()[:]
    banks = [ps_all[:, b * 512:(b + 1) * 512] for b in range(8)]
    free_banks = list(range(8))

    def newbank():
        b = free_banks.pop(0)
        free_banks.append(b)
        return banks[b], ("B", b)

    def holdbank():
        b = free_banks.pop(0)
        return banks[b], ("B", b), b

    def relbank(b):
        free_banks.append(b)

    def holdall():
        assert len(free_banks) == 8
        del free_banks[:]

    def relall():
        free_banks.extend(range(8))

    def rot(lst, name):
        ctr = [0]

        def get():
            i = ctr[0] % len(lst)
            ctr[0] += 1
            return lst[i], (name, i)
        return get

    sqtile = rot(sq_t, "sq")
    qsqtile = rot(qsq_t, "qsq")
    rstile = rot(rs_t, "rs")
    pttile = rot(pt_t, "pt")
    extile = rot(ex_t, "ex")
    rectile = rot(rec_t, "rec")
    rltile = rot(rl_t, "rl")

    def hc(c, t, lo=0, n=SUB):
        return h_sb[:, c * TILE + t * SUB + lo: c * TILE + t * SUB + lo + n]

    def xc(c, t, lo=0, n=SUB):
        return xn_sb[:, c * TILE + t * SUB + lo: c * TILE + t * SUB + lo + n]

    hk = lambda c, t: ("h", c, t)
    xk = lambda c, t: ("xn", c, t)
    pcol = lambda i: par_sb[:, i:i + 1]

    def hidc(j, t):
        return hid_sb[:, j * TILE + t * SUB: j * TILE + (t + 1) * SUB]

    def hidk(j, t):
        return ck(j * 2048 + t * 1024, j * 2048 + (t + 1) * 1024)

    def dc(c, t):
        return d_sb[:, c * TILE + t * SUB: c * TILE + (t + 1) * SUB]

    def dk(c, t):
        return ck(D_OFF + c * 2048 + t * 1024, D_OFF + c * 2048 + (t + 1) * 1024)

    def uk(c, t):
        b = c * 4160
        if t < 0:
            return ck(b, b + 64)
        return ck(b + 64 + t * 2048, b + 64 + (t + 1) * 2048)

    def qc(c, t, is_a):
        if is_a:
            return q67_sb[:, (c - 6) * TILE + t * SUB: (c - 6) * TILE + (t + 1) * SUB]
        return q_sb[:, c * TILE + t * SUB: c * TILE + (t + 1) * SUB]

    def qk(c, t, is_a):
        if is_a:
            b0 = Q67_OFF + (c - 6) * 2048 + t * 1024
        else:
            b0 = c * 2048 + t * 1024
        return ck(b0, b0 + 1024)

    wseq = []
    for l in layers:
        wseq.append(("wmem", l, wmem_d[l], 4096))
    for T in range(n_tiles):
        for l in layers:
            if do_kv and l == kv_layer:
                wseq.append(("wkd", T, wkd_d, 4096))
                wseq.append(("wv", T, wv_d, 2048))
            wseq.append(("win", (T, l, 0), win_d[l, 0], 4096))
            wseq.append(("win", (T, l, 1), win_d[l, 1], 4096))
            if l < N_A:
                wseq.append(("wpool", (T, l, 0), wpool_d[l, 0], 2304))
                wseq.append(("wpool", (T, l, 1), wpool_d[l, 1], 2304))
            wseq.append(("wout", (T, l, 0), wout_d[l, 0], 4096))
            wseq.append(("wout", (T, l, 1), wout_d[l, 1], 4096))
            for hf in range(2):
                for jp in range(4):
                    wseq.append(("wup", (T, l, hf, jp), wup_d[l, hf * 4 + jp], 4096))
                for ip in range(4):
                    wseq.append(("wdn", (T, l, hf, ip), wdn_d[l, hf, ip], 4096))
    w_issued = [0]
    w_used = [0]
    w_released = set()

    def w_pump():
        while w_issued[0] < len(wseq):
            i = w_issued[0]
            if i - NSLOT >= 0 and (i - NSLOT) not in w_released:
                break
            if i >= w_used[0] + NSLOT:
                break
            _, _, ap, n = wseq[i]
            s = i % NSLOT
            P.add("pool", (lambda ap=ap, n=n, s=s: lambda e: e.dma_start(out=ring[s][:, 0:n], in_=ap))(),
                  w=[("slot", s)], dsem=s)
            w_issued[0] += 1

    def wnext(kind, ident):
        i = w_used[0]
        assert wseq[i][0] == kind and wseq[i][1] == ident, (wseq[i][:2], kind, ident)
        w_used[0] += 1
        w_pump()
        assert w_issued[0] > i, "weight piece could not be issued (ring full of unreleased pieces)"
        s = i % NSLOT
        return ring[s], ("slot", s), i

    def wrel(i):
        w_released.add(i)
        w_pump()

    DS_PAR = NSLOT
    DS_X = [NSLOT + 1, NSLOT + 2]
    DS_ST = [NSLOT + 3 + i for i in range(4)]
    N_DSEM = NSLOT + 7
    st_ctr = [0]

    def mm(out, lhsT, rhs, start, stop, r, w, sgc=False):
        if sgc:
            P.add("pe", lambda e: e.matmul(out, lhsT, rhs, start=start, stop=stop, skip_group_check=True), r=r, w=w)
        else:
            P.add("pe", lambda e: e.matmul(out, lhsT, rhs, start=start, stop=stop), r=r, w=w)

    def mm_group(out, pairs, r, w):
        n = len(pairs)

        def fn(e):
            ins = None
            for i, (lt, rh) in enumerate(pairs):
                ins = e.matmul(out, lt, rh, start=(i == 0), stop=(i == n - 1))
            return ins
        P.add("pe", fn, r=r, w=w)

    def act(out, in_, func, r, w, scale=1.0, bias=None):
        if bias is None:
            P.add("act", lambda e: e.activation(out=out, in_=in_, func=func, scale=scale), r=r, w=w)
        else:
            P.add("act", lambda e: e.activation(out=out, in_=in_, func=func, scale=scale, bias=bias), r=r, w=w)

    def tt(out, in0, in1, op, r, w, eng="dve"):
        P.add(eng, lambda e: e.tensor_tensor(out=out, in0=in0, in1=in1, op=op), r=r, w=w)

    def stt(out, in0, scalar, in1, op0, op1, r, w, eng="dve"):
        P.add(eng, lambda e: e.scalar_tensor_tensor(out=out, in0=in0, scalar=scalar, in1=in1, op0=op0, op1=op1),
              r=r, w=w)

    def rstd_from_bank(bank, bkey, n=SUB):
        rs, rsk = rstile()
        act(rs[:, 0:n], bank[:, 0:n], AF.Ln, r=[bkey, "const"], w=[rsk], bias=epsc)
        act(rs[:, 0:n], rs[:, 0:n], AF.Exp, r=[rsk], w=[rsk], scale=-0.5)
        return rs, rsk

    def qnorm_a(bank, bkey, n=SUB):
        sq, sqk = qsqtile()
        act(sq[:, 0:n], bank[:, 0:n], AF.Square, r=[bkey], w=[sqk])
        return sq, sqk

    def qnorm_b(bank, bkey, sq, sqk, gcol, dst, dkeys, n=SUB):
        b2, b2k = newbank()
        mm(b2[:, 0:n], bd_bf, sq[:, 0:n], True, True, r=[sqk, "const"], w=[b2k])
        rs, rsk = rstd_from_bank(b2, b2k, n)
        stt(dst, bank[:, 0:n], gcol, rs[:, 0:n], ALU.mult, ALU.mult, r=[bkey, rsk, "params"], w=dkeys)

    def qnorm(bank, bkey, gcol, dst, dkeys, n=SUB):
        sq, sqk = qnorm_a(bank, bkey, n)
        qnorm_b(bank, bkey, sq, sqk, gcol, dst, dkeys, n)

    def norm_t(gcol0, t):
        bank, bkey = newbank()
        for c in range(8):
            sq, sqk = sqtile()
            act(sq, hc(c, t), AF.Square, r=[hk(c, t)], w=[sqk])
            mm(bank, ones_bf, sq, c == 0, c == 7, r=[sqk, "const"], w=[bkey])
        rs, rsk = rstd_from_bank(bank, bkey)
        for c in range(8):
            stt(xc(c, t), hc(c, t), pcol(gcol0 + c), rs, ALU.mult, ALU.mult,
                r=[hk(c, t), rsk, "params"], w=[xk(c, t)])

    def lagged(items, stage1, stage2, lag):
        pend = []
        for it in items:
            stage1(it)
            pend.append(it)
            if len(pend) > lag:
                stage2(pend.pop(0))
        while pend:
            stage2(pend.pop(0))

    P.add("sp", lambda e: e.dma_start(out=par_sb, in_=par_d), w=["params"], dsem=DS_PAR)
    dist_sb = cview(0, 256, F32)
    mask_sb = cview(1024, 256, F32)
    P.add("sp", lambda e: e.dma_start(out=dist_sb, in_=cst_d[:, 0:256]), w=ck(0, 1024), dsem=DS_PAR)
    P.add("sp", lambda e: e.dma_start(out=mask_sb, in_=cst_d[:, 256:512]), w=ck(1024, 2048), dsem=DS_PAR)
    P.add("sp", lambda e: e.dma_start(out=invcnt, in_=cst_d[:, 512:576]), w=["invcnt"], dsem=DS_PAR)
    P.add("dve", lambda e: e.memset(ones_bf, 1.0 / 1024.0), w=["const"])
    P.add("dve", lambda e: e.memset(bd_bf, 0.0), w=["const"])
    P.add("dve", lambda e: e.memset(bd_bf[0:64, 0:64], 1.0 / 64.0), w=["const"])
    P.add("dve", lambda e: e.memset(bd_bf[64:128, 64:128], 1.0 / 64.0), w=["const"])
    P.add("dve", lambda e: e.memset(ones64, 1.0), w=["const"])
    P.add("dve", lambda e: e.memset(epsc, EPS), w=["const"])
    if has_b:
        act(esink, par_sb[:, PC_SINK:PC_SINK + 12], AF.Exp, r=["params"], w=["esink"])
        for hh in range(12):
            eh_ = etab[:, hh * 256:(hh + 1) * 256]
            act(eh_, dist_sb, AF.Exp, r=ck(0, 1024), w=[("etab", hh)], scale=-SLOPES[hh])
            tt(eh_, eh_, mask_sb, ALU.mult, r=[("etab", hh)] + ck(1024, 2048), w=[("etab", hh)])

    MX_OFF = 2048
    memx = cview(MX_OFF, 8 * 256, F32)
    MN_OFF = MX_OFF + 8192
    memn = cview(MN_OFF, 8 * 256, BF16)
    MR_OFF = MN_OFF + 4096
    memrs = cview(MR_OFF, 256, F32)
    P.add("sp", lambda e: e.dma_start(out=memx.rearrange("p (c n) -> p c n", c=8),
                                      in_=memT_d.rearrange("(c p) n -> p c n", p=128)),
          w=ck(MX_OFF, MX_OFF + 8192), dsem=DS_PAR)
    bank, bkey = newbank()
    for c in range(8):
        sq, sqk = sqtile()
        act(sq[:, 0:256], memx[:, c * 256:(c + 1) * 256], AF.Square, r=ck(MX_OFF, MX_OFF + 8192), w=[sqk])
        mm(bank[:, 0:256], ones_bf, sq[:, 0:256], c == 0, c == 7, r=[sqk, "const"], w=[bkey])
    act(memrs, bank[:, 0:256], AF.Ln, r=[bkey, "const"], w=ck(MR_OFF, MR_OFF + 1024), bias=epsc)
    act(memrs, memrs, AF.Exp, r=ck(MR_OFF, MR_OFF + 1024), w=ck(MR_OFF, MR_OFF + 1024), scale=-0.5)
    for l in layers:
        for c in range(8):
            stt(memn[:, c * 256:(c + 1) * 256], memx[:, c * 256:(c + 1) * 256], pcol(PC_MEMN + 8 * l + c), memrs,
                ALU.mult, ALU.mult, r=ck(MX_OFF, MX_OFF + 8192) + ck(MR_OFF, MR_OFF + 1024) + ["params"],
                w=ck(MN_OFF, MN_OFF + 4096))
        ws, wk, wi = wnext("wmem", l)
        for cmi in range(2):
            bank, bkey = newbank()
            mm_group(bank[:, 0:256],
                     [(ws[:, kc * 512 + cmi * 128: kc * 512 + (cmi + 1) * 128], memn[:, kc * 256:(kc + 1) * 256])
                      for kc in range(8)], r=[wk] + ck(MN_OFF, MN_OFF + 4096), w=[bkey])
            o_ = (l * 2 + cmi) * 256
            qnorm(bank, bkey, pcol(PC_MKN + l), mk_sb[:, o_:o_ + 256], [("mk", l, cmi)], n=256)
        for mc in range(2):
            bank, bkey = newbank()
            mm_group(bank[:, 0:256],
                     [(memn[:, kc * 256 + mc * 128: kc * 256 + (mc + 1) * 128], ws[:, kc * 512 + 256: kc * 512 + 512])
                      for kc in range(8)], r=[wk] + ck(MN_OFF, MN_OFF + 4096), w=[bkey])
            o_ = (l * 2 + mc) * 256
            P.add("act", (lambda o_=o_, bank=bank: lambda e: e.copy(out=mv_sb[:, o_:o_ + 256], in_=bank[:, 0:256]))(),
                  r=[bkey], w=[("mv", l, mc)])
        wrel(wi)

    def in_t(T, l, t, wsl, hook=None):
        is_a = l < N_A
        st = {}

        def s1(c):
            ws, wk, _ = wsl[c // 4]
            bank, bkey = newbank()
            mm_group(bank, [(ws[:, kc * 512 + (c % 4) * 128: kc * 512 + (c % 4 + 1) * 128], xc(kc, t))
                            for kc in range(8)], r=[wk] + [xk(kc, t) for kc in range(8)], w=[bkey])
            if is_a and c < 6:
                dst = u_sb[:, c * 1040 + 16 + t * SUB: c * 1040 + 16 + (t + 1) * SUB]
                P.add("act", (lambda dst=dst, bank=bank: lambda e: e.copy(out=dst, in_=bank))(), r=[bkey], w=uk(c, t))
                st[c] = None
            else:
                sq, sqk = qnorm_a(bank, bkey)
                st[c] = (bank, bkey, sq, sqk)
            if hook is not None and c == 1:
                hook()

        def s2(c):
            if st[c] is None:
                return
            bank, bkey, sq, sqk = st[c]
            if c >= 6:
                qnorm_b(bank, bkey, sq, sqk, pcol(PC_MQN + l), qc(c, t, is_a), qk(c, t, is_a))
            else:
                qnorm_b(bank, bkey, sq, sqk, pcol(PC_QN + (l - N_A)), qc(c, t, False), qk(c, t, False))

        lagged(list(range(8)), s1, s2, 1)

    POOL_PIECES = {0: [(0, 0, 128), (1, 0, 64)], 1: [(1, 64, 128), (2, 0, 128)],
                   2: [(3, 0, 128), (4, 0, 64)], 3: [(4, 64, 128), (5, 0, 128)]}
    POOL_NZ = {0: [0, 1], 1: [0, 1, 2], 2: [1, 2], 3: [3, 4], 4: [3, 4, 5], 5: [4, 5]}
    u3 = u_sb.rearrange("p (c n) -> p c n", c=6)

    def pool_halo(T, l):
        cr3 = carry[:, l * 96:(l + 1) * 96].rearrange("p (c n) -> p c n", c=6)
        hkeys = [k for c in range(6) for k in uk(c, -1)]
        if T == 0:
            P.add("dve", lambda e: e.memset(u3[:, :, 0:16], 0.0), w=hkeys)
        else:
            P.add("dve", lambda e: e.tensor_copy(out=u3[:, :, 0:16], in_=cr3), r=[("carry", l)], w=hkeys)

    def pool_dve(T, l, t, eng="dve"):
        sabx = sab if eng == "dve" else sab2
        sname = "sab" if eng == "dve" else "sab2"
        for g in range(4):
            W = POOL_W[g]
            for (c, plo, phi) in POOL_PIECES[g]:
                base = c * 1040 + t * SUB
                src = u_sb[plo:phi, base: base + 528]
                srck = uk(c, t) + (uk(c, t - 1) if t > 0 else uk(c, -1))
                cur = src
                curk = srck
                k = 1
                si = 0
                while k < W:
                    lo = 16 - (W - 2 * k)
                    o = sabx[si][plo:phi, :]
                    ok = [(sname, si)]
                    tt(o[:, lo:528], cur[:, lo:528], cur[:, lo - k:528 - k], ALU.add, r=curk, w=ok, eng=eng)
                    cur, curk = o, ok
                    si ^= 1
                    k *= 2
                dst = d_sb[plo:phi, c * TILE + t * SUB: c * TILE + (t + 1) * SUB]
                stt(dst, cur[:, 16:528], 1.0 / W, src[:, 16:528], ALU.mult, ALU.subtract,
                    r=curk + srck, w=dk(c, t), eng="dve")
                if T == 0 and t == 0:
                    ic = invcnt[plo:phi, 16 * g:16 * (g + 1)]
                    tt(cur[:, 16:32], cur[:, 16:32], ic, ALU.mult, r=curk + ["invcnt"], w=curk, eng=eng)
                    tt(dst[:, 0:16], cur[:, 16:32], src[:, 16:32], ALU.subtract, r=curk + srck, w=dk(c, t), eng=eng)
        if t == 1:
            cr3 = carry[:, l * 96:(l + 1) * 96].rearrange("p (c n) -> p c n", c=6)
            P.add(eng, lambda e: e.tensor_copy(out=cr3, in_=u3[:, :, 1024:1040]),
                  r=[k for c in range(6) for k in uk(c, 1)], w=[("carry", l)])

    def pool_mm(T, l, t, wps):
        for oc in range(6):
            ws, wk, _ = wps[oc // 3]
            bank, bkey = newbank()
            rk = [wk]
            for kc in POOL_NZ[oc]:
                rk += dk(kc, t)
            mm_group(bank, [(ws[:, kc * 384 + (oc % 3) * 128: kc * 384 + (oc % 3 + 1) * 128], dc(kc, t))
                            for kc in POOL_NZ[oc]], r=rk, w=[bkey])
            act(xc(oc, t), bank, AF.Copy, r=[bkey, "params"], w=[xk(oc, t)], scale=pcol(PC_PSC + 6 * l + oc))

    def mem_t(T, l, t):
        is_a = l < N_A
        for cmi in range(2):
            ob, obk, obi = holdbank()
            db, dbk, dbi = holdbank()
            st = {}

            def s1(it, cmi=cmi):
                hh, mc = it
                po = 64 * hh
                sb, sbk = newbank()
                o_ = (l * 2 + cmi) * 256 + mc * 128
                qa = qc(6 + cmi, t, is_a)
                mm(sb, mk_sb[po:po + 64, o_:o_ + 128], qa[po:po + 64, :], True, True,
                   r=[("mk", l, cmi)] + qk(6 + cmi, t, is_a), w=[sbk])
                pt, ptk = pttile()
                act(pt, sb, AF.Exp, r=[sbk], w=[ptk], scale=0.125)
                st[it] = (pt, ptk)

            def s2(it, cmi=cmi, ob=ob, obk=obk, db=db, dbk=dbk):
                hh, mc = it
                po = 64 * hh
                hm = 2 * cmi + hh
                pt, ptk = st[it]
                o_ = (l * 2 + mc) * 256 + hm * 64
                mm(ob[po:po + 64, :], mv_sb[:, o_:o_ + 64], pt, mc == 0, mc == 1, r=[ptk, ("mv", l, mc)], w=[obk])
                mm(db[po:po + 64, :], ones64, pt, mc == 0, mc == 1, r=[ptk, "const"], w=[dbk])

            lagged([(hh, mc) for mc in range(2) for hh in range(2)], s1, s2, 3)
            rec, reck = rectile()
            act(rec, db, AF.Ln, r=[dbk], w=[reck])
            act(rec, rec, AF.Exp, r=[reck], w=[reck], scale=-1.0)
            tt(xc(6 + cmi, t), ob, rec, ALU.mult, r=[obk, reck], w=[xk(6 + cmi, t)])
            relbank(obi)
            relbank(dbi)

    def swa_phase(T, l):
        jl = l - N_A
        holdall()
        hb_ = [(banks[b], ("B", b), b) for b in range(4)]
        sets = [(hb_[0], hb_[1]), (hb_[2], hb_[3])]
        spair = [(ps_all[:, 2048:3072], [("B", 4), ("B", 5)]), (ps_all[:, 3072:4096], [("B", 6), ("B", 7)])]
        sp_ctr = [0]
        etab3 = etab.rearrange("p (h n) -> p h n", h=12)
        items = []
        first_j = {}
        last_j = {}
        for c in range(6):
            for t in range(2):
                js = [j for j in range(4 * t - 1, 4 * t + 4) if not (T == 0 and j == -1)]
                for j in js:
                    items.append((c, t, j))
                first_j[(c, t)] = js[0]
                last_j[(c, t)] = js[-1]
        st = {}
        keep = {}

        def s1(it):
            c, t, j = it
            if t == 1 and j == 3:
                st[it] = keep[c]
                return
            b0, b1 = max(j, 0), min(j + 1, 7)
            wdt = (b1 - b0 + 1) * 128
            tc0 = 128 if j == -1 else 0
            sb, sbks = spair[sp_ctr[0] % 2]
            sp_ctr[0] += 1
            for hh in range(2):
                h_ = 2 * c + hh
                kvh = h_ // 3
                po = 64 * hh
                rk = [("K", kvh, j + 1)]
                for b in range(b0, b1 + 1):
                    rk += qk(c, b // 4, False)
                mm(sb[:, hh * 512: hh * 512 + wdt],
                   kd_sb[po:po + 64, kvh * 1152 + (j + 1) * 128: kvh * 1152 + (j + 2) * 128],
                   q_sb[po:po + 64, c * TILE + b0 * 128: c * TILE + (b1 + 1) * 128], True, True, r=rk,
                   w=[sbks[hh]])
            ex, exk = extile()
            if t == 0 and j == 3:
                pt, ptk = keep_t[c % 2], ("keep", c % 2)
                keep[c] = (pt, ptk)
            else:
                pt, ptk = pttile()
            sbv = sb.rearrange("p (h n) -> p h n", h=2)[:, :, 0:wdt]
            if wdt == 256:
                exv, ptv = ex, pt
                exv_a = ex.rearrange("p (h n) -> p h n", h=2)
            else:
                exv = ex.rearrange("p (h n) -> p h n", h=2)[:, :, 0:wdt]
                ptv = pt.rearrange("p (h n) -> p h n", h=2)[:, :, 0:wdt]
                exv_a = exv
            act(exv_a, sbv, AF.Exp, r=sbks, w=[exk], scale=0.125)
            tt(ptv, exv, etab3[:, 2 * c:2 * c + 2, tc0:tc0 + wdt] if wdt != 256 else
               etab[:, 2 * c * 256:(2 * c + 2) * 256], ALU.mult,
               r=[exk, ("etab", 2 * c), ("etab", 2 * c + 1)], w=[ptk])
            st[it] = (pt, ptk)

        def s2(it):
            c, t, j = it
            (ob, obk, _), (db, dbk, _) = sets[t]
            pt, ptk = st[it]
            vk = ("V", j + 1)
            if j == 4 * t - 1:
                pc0, pw, oc0 = (0 if j == -1 else 128), 128, 0
            elif j == 4 * t + 3:
                pc0, pw, oc0 = 0, 128, 384
            else:
                pc0, pw, oc0 = 0, 256, (j - 4 * t) * 128
            first = (j == first_j[(c, t)])
            for hh in range(2):
                kvh = (2 * c + hh) // 3
                po = 64 * hh
                vap = v_sb[:, (j + 1) * 256 + kvh * 64: (j + 1) * 256 + (kvh + 1) * 64]
                mm(ob[po:po + 64, oc0:oc0 + pw], vap, pt[:, hh * 256 + pc0: hh * 256 + pc0 + pw], first, True,
                   r=[ptk, vk], w=[obk], sgc=True)
            for hh in range(2):
                po = 64 * hh
                mm(db[po:po + 64, oc0:oc0 + pw], ones64, pt[:, hh * 256 + pc0: hh * 256 + pc0 + pw], first, True,
                   r=[ptk, "const"], w=[dbk], sgc=True)
            if j == last_j[(c, t)]:
                rec, reck = rectile()
                act(rec, db, AF.Ln, r=[dbk, "esink"], w=[reck], bias=esink[:, jl * 6 + c: jl * 6 + c + 1])
                act(rec, rec, AF.Exp, r=[reck], w=[reck], scale=-1.0)
                tt(xc(c, t), ob, rec, ALU.mult, r=[obk, reck], w=[xk(c, t)])

        lagged(items, s1, s2, 2)
        relall()

    def out_phase(T, l, hook=None):
        w0 = wnext("wout", (T, l, 0))
        w1 = wnext("wout", (T, l, 1))
        wsl = [w0, w1]
        for t in range(2):
            for c in range(8):
                ws, wk, _ = wsl[c // 4]
                bank, bkey = newbank()
                mm_group(bank, [(ws[:, kc * 512 + (c % 4) * 128: kc * 512 + (c % 4 + 1) * 128], xc(kc, t))
                                for kc in range(8)], r=[wk] + [xk(kc, t) for kc in range(8)], w=[bkey])
                tt(hc(c, t), bank, hc(c, t), ALU.add, r=[bkey, hk(c, t)], w=[hk(c, t)])
                if hook is not None and t == 1 and c == 1:
                    hook()
        wrel(w0[2])
        wrel(w1[2])

    def up_group(ws, wk, jj, j, t):
        bank, bkey = newbank()
        mm_group(bank, [(ws[:, kc * 512 + jj * 128: kc * 512 + (jj + 1) * 128], xc(kc, t))
                        for kc in range(8)], r=[wk] + [xk(kc, t) for kc in range(8)], w=[bkey])
        rl, rlk = rltile()
        act(rl, bank, AF.Relu, r=[bkey], w=[rlk])
        tt(hidc(j, t), rl, rl, ALU.mult, r=[rlk], w=hidk(j, t))

    def down_group(T, ws, wk, ii, i, t, store):
        bank, bkey = newbank()
        rk = [wk]
        for kc in range(16):
            rk += hidk(kc, t)
        mm_group(bank, [(ws[:, kc * 256 + ii * 128: kc * 256 + (ii + 1) * 128], hidc(kc, t))
                        for kc in range(16)], r=rk, w=[bkey])
        tt(hc(i, t), bank, hc(i, t), ALU.add, r=[bkey, hk(i, t)], w=[hk(i, t)])
        if store:
            ds = DS_ST[st_ctr[0] % 4]
            st_ctr[0] += 1
            dst = yT_d[i * 128:(i + 1) * 128, T * TILE + t * SUB: T * TILE + (t + 1) * SUB]
            P.add("sp", (lambda dst=dst, src=hc(i, t): lambda e: e.dma_start(out=dst, in_=src))(),
                  r=[hk(i, t)], w=[("y", T, i, t)], dsem=ds)

    def mlp_phase(T, l, last, norm2_t1, next_norm_t0, after_t0=None):
        for hf in range(2):
            for pair in range(2):
                wp = [wnext("wup", (T, l, hf, 2 * pair + q)) for q in range(2)]
                for t in range(2):
                    for q in range(2):
                        ws, wk, _ = wp[q]
                        for jj in range(4):
                            up_group(ws, wk, jj, (2 * pair + q) * 4 + jj, t)
                            if hf == 0 and pair == 0 and t == 0 and q == 0 and jj == 1 and norm2_t1 is not None:
                                norm2_t1()
                wrel(wp[0][2])
                wrel(wp[1][2])
            if hf == 0:
                for ip in range(4):
                    ws, wk, wi = wnext("wdn", (T, l, hf, ip))
                    for ii in range(2):
                        for t in range(2):
                            down_group(T, ws, wk, ii, ip * 2 + ii, t, False)
                    wrel(wi)
            else:
                wd = [wnext("wdn", (T, l, hf, ip)) for ip in range(4)]
                for t in range(2):
                    if t == 1 and after_t0 is not None:
                        after_t0()
                    for ip in range(4):
                        ws, wk, _ = wd[ip]
                        for ii in range(2):
                            down_group(T, ws, wk, ii, ip * 2 + ii, t, last)
                            hip = 2 if after_t0 is not None else 0
                            if t == 1 and ip == hip and ii == 1 and next_norm_t0 is not None:
                                next_norm_t0()
                for x_ in wd:
                    wrel(x_[2])

    def kv_phase(T):
        if T > 0:
            k3 = kd_sb.rearrange("p (k n) -> p k n", k=4)
            P.add("dve", lambda e: e.tensor_copy(out=k3[:, :, 0:128], in_=k3[:, :, 1024:1152]),
                  r=[("K", kvh, 8) for kvh in range(4)], w=[("K", kvh, 0) for kvh in range(4)])
            P.add("dve", lambda e: e.tensor_copy(out=v_sb[:, 0:256], in_=v_sb[:, 8 * 256:9 * 256]),
                  r=[("V", 8)], w=[("V", 0)])
        ws, wk, wi = wnext("wkd", T)
        for t in range(2):
            st = {}

            def s1(kvh, t=t):
                bank, bkey = newbank()
                mm_group(bank, [(ws[:, kc * 512 + kvh * 128: kc * 512 + (kvh + 1) * 128], xc(kc, t))
                                for kc in range(8)], r=[wk] + [xk(kc, t) for kc in range(8)], w=[bkey])
                sq, sqk = qnorm_a(bank, bkey)
                st[kvh] = (bank, bkey, sq, sqk)
                if t == 0 and kvh == 1:
                    norm_t(PC_KVN, 1)

            def s2(kvh, t=t):
                bank, bkey, sq, sqk = st[kvh]
                dst = kd_sb[:, kvh * 1152 + 128 + t * SUB: kvh * 1152 + 128 + (t + 1) * SUB]
                qnorm_b(bank, bkey, sq, sqk, pcol(PC_KN), dst, [("K", kvh, 1 + 4 * t + b) for b in range(4)])

            lagged(list(range(4)), s1, s2, 1)
        wrel(wi)
        ws, wk, wi = wnext("wv", T)
        for n in range(8):
            t = n // 4
            bank, bkey = newbank()
            mm_group(bank[:, 0:256], [(xc(kc, t, (n % 4) * 128, 128), ws[:, kc * 256:(kc + 1) * 256])
                                      for kc in range(8)], r=[wk] + [xk(kc, t) for kc in range(8)], w=[bkey])
            P.add("act", (lambda n=n, bank=bank: lambda e: e.copy(out=v_sb[:, (n + 1) * 256:(n + 2) * 256],
                                                                     in_=bank[:, 0:256]))(),
                  r=[bkey], w=[("V", n + 1)])
        wrel(wi)

    h3 = h_sb.rearrange("p (c n) -> p c n", c=8)
    x3 = xT_d.rearrange("(c p) n -> p c n", p=128)

    def first_norm_col(l):
        return PC_KVN if (do_kv and l == kv_layer) else PC_NMIX + 8 * l

    def xload(T, t):
        src = x3[:, :, T * TILE + t * SUB: T * TILE + (t + 1) * SUB]
        dst = h3[:, :, t * SUB:(t + 1) * SUB]
        P.add("sp", (lambda dst=dst, src=src: lambda e: e.dma_start(out=dst, in_=src))(),
              w=[hk(c, t) for c in range(8)], dsem=DS_X[t])

    for T in range(n_tiles):
        if T == 0:
            xload(T, 0)
            xload(T, 1)
            norm_t(first_norm_col(layers[0]), 0)
        else:
            xload(T, 1)
        for li, l in enumerate(layers):
            is_a = l < N_A
            if do_kv and l == kv_layer:
                kv_phase(T)
                norm_t(PC_NMIX + 8 * l, 0)
            w0 = wnext("win", (T, l, 0))
            w1 = wnext("win", (T, l, 1))
            wsl = [w0, w1]
            in_t(T, l, 0, wsl, hook=(lambda l=l: norm_t(PC_NMIX + 8 * l, 1)))
            if is_a:
                pool_halo(T, l)
                pool_dve(T, l, 0)
            in_t(T, l, 1, wsl)
            wrel(w0[2])
            wrel(w1[2])
            if is_a:
                wps = [wnext("wpool", (T, l, 0)), wnext("wpool", (T, l, 1))]
                mem_t(T, l, 0)
                pool_mm(T, l, 0, wps)
                pool_dve(T, l, 1, eng=POOL_ENG_T1)
                mem_t(T, l, 1)
                pool_mm(T, l, 1, wps)
                wrel(wps[0][2])
                wrel(wps[1][2])
            else:
                mem_t(T, l, 0)
                swa_phase(T, l)
                mem_t(T, l, 1)
            out_phase(T, l, hook=(lambda l=l: norm_t(PC_NMLP + 8 * l, 0)))
            last = (l == layers[-1])
            nxt = None
            aft = None
            if not last:
                nxt = (lambda l=l: norm_t(first_norm_col(l + 1), 0))
            elif T + 1 < n_tiles:
                aft = (lambda T=T: xload(T + 1, 0))
                nxt = (lambda: norm_t(first_norm_col(layers[0]), 0))
            mlp_phase(T, l, last, (lambda l=l: norm_t(PC_NMLP + 8 * l, 1)), nxt, aft)
    P.add("sp", None, r=[("y", T, i, t) for T in range(n_tiles) for i in range(8) for t in range(2)])
    assert w_used[0] == len(wseq)

    sem_cms = {e: nc.semaphore("s_" + e) for e in ENGS}
    esems = {e: c_.__enter__() for e, c_ in sem_cms.items()}
    dsem_cms = [nc.semaphore(f"d{i}") for i in range(N_DSEM)]
    dsems = [c_.__enter__() for c_ in dsem_cms]
    with nc.Block() as block:
        P.emit(block, esems, dsems)
    for c_ in dsem_cms[::-1]:
        c_.__exit__(None, None, None)
    for c_ in list(sem_cms.values())[::-1]:
        c_.__exit__(None, None, None)
    for c_ in pcms[::-1]:
        c_.__exit__(None, None, None)
    cm.__exit__(None, None, None)
    return nc


_NC_CACHE = {}


def _get_nc(l0, l1):
    key = (l0, l1)
    if key not in _NC_CACHE:
        _NC_CACHE[key] = build(l0, l1)
    return _NC_CACHE[key]


LAUNCHES = [(0, 4)]


def kernel(**inputs):
    x = np.asarray(inputs["x"], dtype=np.float32)
    mem = np.asarray(inputs["mem"], dtype=np.float32)
    B = x.shape[0]
    sh = _prep_shared(inputs)
    cur = [np.ascontiguousarray(x[b].T) for b in range(B)]
    memT = [np.ascontiguousarray(mem[b].T) for b in range(B)]
    for (l0, l1) in LAUNCHES:
        nc = _get_nc(l0, l1)
        in_maps = []
        for b in range(B):
            m = dict(sh)
            m["xT"] = cur[b]
            m["memT"] = memT[b]
            in_maps.append(m)
        res = run_bass_kernel_spmd(nc, in_maps, core_ids=list(range(B)))
        cur = [np.asarray(res.results[b]["yT"]) for b in range(B)]
    out = np.stack([np.ascontiguousarray(cur[b].T) for b in range(B)]).astype(np.float32)
    return out
```
